# Optimizing a Trainium2 kernel written in Bass

```python
import math
import jax, jax.numpy as jnp
from jax import lax
import numpy as np

D_MODEL = 1024
BATCH = 2
SEQ = 8192
DEPTH = 1

PLE_DIM = 256
D_MIX = D_MODEL
RET_HEADS = 4
RET_HEAD_DIM = 128
RET_WIDTH = RET_HEADS * RET_HEAD_DIM
RET_CHUNK = 128
MLA_HEADS = 8
MLA_NOPE_DIM = 64
MLA_ROPE_DIM = 32
MLA_QK_DIM = MLA_NOPE_DIM + MLA_ROPE_DIM
MLA_V_DIM = 64
MLA_WIDTH = MLA_HEADS * MLA_V_DIM
MLA_Q_LORA = 384
MLA_KV_LORA = 256
Q_BLOCK = 128
IN_COLS = 4 * RET_WIDTH + MLA_Q_LORA + MLA_KV_LORA + MLA_ROPE_DIM
D_FF = ((8 * D_MODEL // 3 + 255) // 256) * 256
ROPE_BASE = 10000.0
EPS = 1e-6

kernel_name = "hybrid_retention_mla_parallel_heads"


def rmsnorm(x, w):
    xf = x.astype(jnp.float32)
    y = xf * lax.rsqrt(jnp.mean(xf * xf, axis=-1, keepdims=True) + EPS)
    return (y * w.astype(jnp.float32)).astype(x.dtype)


def head_groupnorm(x, w):
    xf = x.astype(jnp.float32)
    mu = jnp.mean(xf, axis=-1, keepdims=True)
    var = jnp.mean(jnp.square(xf - mu), axis=-1, keepdims=True)
    y = (xf - mu) * lax.rsqrt(var + EPS)
    B, S, H, d = x.shape
    return (y.reshape(B, S, H * d) * w.astype(jnp.float32)).astype(x.dtype)


def rope(x, positions):
    d = x.shape[-1]
    half = d // 2
    inv = 1.0 / (ROPE_BASE ** (jnp.arange(half, dtype=jnp.float32) / half))
    ang = positions.astype(jnp.float32)[..., None] * inv
    cos = jnp.cos(ang)[:, :, None, :].astype(x.dtype)
    sin = jnp.sin(ang)[:, :, None, :].astype(x.dtype)
    x1, x2 = x[..., :half], x[..., half:]
    return jnp.concatenate([x1 * cos - x2 * sin, x2 * cos + x1 * sin], axis=-1)


def retention_chunkwise(q, k, v):
    B, S, H, d = q.shape
    C = RET_CHUNK
    N = S // C
    dt = q.dtype
    log_g = jnp.log(1.0 - 2.0 ** (-5.0 - jnp.arange(H, dtype=jnp.float32)))
    j = jnp.arange(C, dtype=jnp.float32)
    diff = j[:, None] - j[None, :]
    D = jnp.where(diff[None] >= 0, jnp.exp(jnp.maximum(diff, 0.0)[None] * log_g[:, None, None]), 0.0).astype(dt)
    zeta = jnp.exp((C - 1 - j)[None, :] * log_g[:, None]).astype(dt)
    xi = jnp.exp((j + 1)[None, :] * log_g[:, None]).astype(dt)
    g_chunk = jnp.exp(C * log_g).astype(dt)

    qc = q.reshape(B, N, C, H, d)
    kc = k.reshape(B, N, C, H, d)
    vc = v.reshape(B, N, C, H, d)
    scores = jnp.einsum('bnchd,bnmhd->bnhcm', qc, kc) * D[None, None]
    inner = jnp.einsum('bnhcm,bnmhe->bnche', scores, vc)
    U = jnp.einsum('bnmhd,bnmhe,hm->nbhde', kc, vc, zeta)

    def step(R, u):
        return g_chunk[None, :, None, None] * R + u, R

    _, R_prev = lax.scan(step, jnp.zeros_like(U[0]), U)
    cross = jnp.einsum('bnchd,nbhde->bnche', qc, R_prev) * xi.T[None, None, :, :, None]
    return (inner + cross).reshape(B, S, H, d)


def mla_causal(q, k, v):
    B, S, H, dqk = q.shape
    dv = v.shape[-1]
    NB = S // Q_BLOCK
    scale = 1.0 / math.sqrt(dqk)
    kt = k.transpose(0, 2, 1, 3)
    vt = v.transpose(0, 2, 1, 3)
    qb = q.reshape(B, NB, Q_BLOCK, H, dqk).transpose(1, 0, 3, 2, 4)
    kpos = jnp.arange(S)

    def block(args):
        qi, bi = args
        s = jnp.einsum('bhqd,bhkd->bhqk', qi, kt).astype(jnp.float32) * scale
        qpos = bi * Q_BLOCK + jnp.arange(Q_BLOCK)
        mask = kpos[None, :] <= qpos[:, None]
        s = jnp.where(mask[None, None], s, -1e30)
        pr = jax.nn.softmax(s, axis=-1).astype(vt.dtype)
        return jnp.einsum('bhqk,bhkd->bhqd', pr, vt)

    out = lax.map(block, (qb, jnp.arange(NB)))
    return out.transpose(1, 0, 3, 2, 4).reshape(B, S, H * dv)


def token_mixer(xn, positions, w_in, ret_gn_w, mla_q_norm, w_uq, mla_kv_norm, w_ukv, w_o):
    B, S, _ = xn.shape
    proj = xn @ w_in
    o = 0
    rq = proj[..., o:o + RET_WIDTH]; o += RET_WIDTH
    rk = proj[..., o:o + RET_WIDTH]; o += RET_WIDTH
    rv = proj[..., o:o + RET_WIDTH]; o += RET_WIDTH
    rg = proj[..., o:o + RET_WIDTH]; o += RET_WIDTH
    cq = proj[..., o:o + MLA_Q_LORA]; o += MLA_Q_LORA
    ckv = proj[..., o:o + MLA_KV_LORA]; o += MLA_KV_LORA
    kr = proj[..., o:o + MLA_ROPE_DIM]

    shp = (B, S, RET_HEADS, RET_HEAD_DIM)
    rq = rope(rq.reshape(shp), positions)
    rk = rope(rk.reshape(shp), positions) * (RET_HEAD_DIM ** -0.5)
    ry = retention_chunkwise(rq, rk, rv.reshape(shp))
    ret_out = jax.nn.silu(rg) * head_groupnorm(ry, ret_gn_w)

    qh = (rmsnorm(cq, mla_q_norm) @ w_uq).reshape(B, S, MLA_HEADS, MLA_QK_DIM)
    q = jnp.concatenate([qh[..., :MLA_NOPE_DIM], rope(qh[..., MLA_NOPE_DIM:], positions)], axis=-1)
    kvh = (rmsnorm(ckv, mla_kv_norm) @ w_ukv).reshape(B, S, MLA_HEADS, MLA_NOPE_DIM + MLA_V_DIM)
    k_rope = rope(kr[:, :, None, :], positions)
    k = jnp.concatenate([kvh[..., :MLA_NOPE_DIM],
                         jnp.broadcast_to(k_rope, (B, S, MLA_HEADS, MLA_ROPE_DIM))], axis=-1)
    v = kvh[..., MLA_NOPE_DIM:]
    mla_out = mla_causal(q, k, v)

    return jnp.concatenate([ret_out, mla_out], axis=-1) @ w_o


def setup_inputs(seed: int = 0) -> dict:
    key = jax.random.key(seed)
    ks = jax.random.split(key, 24)
    L = DEPTH

    def nrm(k, shape, fan_in):
        return jax.random.normal(k, shape, jnp.float32) * (fan_in ** -0.5)

    def gain(k, shape):
        return 1.0 + 0.05 * jax.random.normal(k, shape, jnp.float32)

    return {
        "x": jax.random.normal(ks[0], (BATCH, SEQ, D_MODEL), jnp.float32),
        "p": jax.random.normal(ks[1], (DEPTH, BATCH, SEQ, PLE_DIM), jnp.float32),
        "positions": jnp.broadcast_to(jnp.arange(SEQ, dtype=jnp.int32)[None], (BATCH, SEQ)),
        "pre_mix_norm": gain(ks[2], (L, D_MODEL)),
        "w_in": nrm(ks[3], (L, D_MODEL, IN_COLS), D_MODEL),
        "ret_gn_w": gain(ks[4], (L, RET_WIDTH)),
        "mla_q_norm": gain(ks[5], (L, MLA_Q_LORA)),
        "w_uq": nrm(ks[6], (L, MLA_Q_LORA, MLA_HEADS * MLA_QK_DIM), MLA_Q_LORA),
        "mla_kv_norm": gain(ks[7], (L, MLA_KV_LORA)),
        "w_ukv": nrm(ks[8], (L, MLA_KV_LORA, MLA_HEADS * (MLA_NOPE_DIM + MLA_V_DIM)), MLA_KV_LORA),
        "w_o": nrm(ks[9], (L, D_MIX, D_MODEL), D_MIX),
        "post_mix_norm": gain(ks[10], (L, D_MODEL)),
        "pre_ffn_norm": gain(ks[11], (L, D_MODEL)),
        "w_gate": nrm(ks[12], (L, D_MODEL, D_FF), D_MODEL),
        "w_up": nrm(ks[13], (L, D_MODEL, D_FF), D_MODEL),
        "w_down": nrm(ks[14], (L, D_FF, D_MODEL), D_FF),
        "post_ffn_norm": gain(ks[15], (L, D_MODEL)),
        "w_ple_proj": nrm(ks[16], (L, PLE_DIM, D_MODEL), PLE_DIM),
        "ple_norm": gain(ks[17], (L, D_MODEL)),
        "w_ple_gate": nrm(ks[18], (L, D_MODEL, D_MODEL), D_MODEL),
        "b_ple_gate": 0.02 * jax.random.normal(ks[19], (L, D_MODEL), jnp.float32),
    }


def reference(x, p, positions, pre_mix_norm, w_in, ret_gn_w, mla_q_norm, w_uq, mla_kv_norm,
              w_ukv, w_o, post_mix_norm, pre_ffn_norm, w_gate, w_up, w_down, post_ffn_norm,
              w_ple_proj, ple_norm, w_ple_gate, b_ple_gate):
    h = x
    for i in range(DEPTH):
        xn = rmsnorm(h, pre_mix_norm[i])
        mix = token_mixer(xn, positions, w_in[i], ret_gn_w[i], mla_q_norm[i], w_uq[i],
                          mla_kv_norm[i], w_ukv[i], w_o[i])
        h = h + rmsnorm(mix, post_mix_norm[i])
        hn = rmsnorm(h, pre_ffn_norm[i])
        ff = (jax.nn.silu(hn @ w_gate[i]) * (hn @ w_up[i])) @ w_down[i]
        h = h + rmsnorm(ff, post_ffn_norm[i])
        e = rmsnorm(p[i] @ w_ple_proj[i], ple_norm[i])
        gate = jax.nn.sigmoid(h @ w_ple_gate[i] + b_ple_gate[i])
        h = h + e * gate
    return h
```

```python
import contextlib
import numpy as np
import concourse.bass as bass
import concourse.mybir as mybir
from concourse.bass_utils import run_bass_kernel_spmd

F32 = mybir.dt.float32
BF16 = mybir.dt.bfloat16
I32 = mybir.dt.int32
AF = mybir.ActivationFunctionType
ALU = mybir.AluOpType

ENGS = ("tensor", "vector", "scalar", "gpsimd", "sync")


class Buf:
    __slots__ = ("name", "last_w", "readers", "wsem", "wcnt", "rsem", "rcnt", "excl")

    def __init__(self, name, excl=False):
        self.name = name
        self.excl = excl
        self.last_w = None
        self.readers = []
        self.wsem = None
        self.wcnt = 0
        self.rsem = None
        self.rcnt = 0


class Op:
    __slots__ = ("eng", "seq", "fn", "waits", "sem", "val", "is_dma", "signal", "is_pe_acc")

    def __init__(self, eng, seq, fn, is_dma):
        self.eng = eng
        self.seq = seq
        self.fn = fn
        self.waits = []
        self.sem = None
        self.val = None
        self.is_dma = is_dma
        self.signal = False


class Prog:
    def __init__(self, nc):
        self.nc = nc
        self.stack = contextlib.ExitStack()
        self.ops = {e: [] for e in ENGS}
        self.esem = {}
        self.nsem = 0
        for e in ENGS:
            if e != "sync":
                self.esem[e] = self.new_sem("e_" + e)
        self.waited = {e: {p: -1 for p in ENGS} for e in ENGS}
        self.waited_sem = {e: {} for e in ENGS}
        self.out_dmas = []
        self.dma_latest = {}

    def new_sem(self, name):
        self.nsem += 1
        return self.stack.enter_context(self.nc.semaphore(name))

    def sbuf(self, name, shape, dtype):
        return self.stack.enter_context(self.nc.sbuf_tensor(name, shape, dtype))

    def psum(self, name, shape, dtype):
        return self.stack.enter_context(self.nc.psum_tensor(name, shape, dtype))

    def _dep(self, op, prod, same_eng_ok):
        if prod is None or prod is op:
            return
        ce = op.eng
        if prod.is_dma:
            key = id(prod.sem)
            cur = self.waited_sem[ce].get(key, 0)
            if prod.val > cur:
                self.waited_sem[ce][key] = prod.val
                op.waits.append(prod)
            return
        pe = prod.eng
        if pe == ce and not op.is_dma:
            if same_eng_ok:
                return
        if prod.seq > self.waited[ce][pe]:
            self.waited[ce][pe] = prod.seq
            prod.signal = True
            op.waits.append(prod)

    def add(self, eng, fn, reads=(), writes=(), pe_acc=False):
        op = Op(eng, len(self.ops[eng]), fn, False)
        for b in reads:
            self._dep(op, b.last_w, False)
            if b.excl:
                for r in b.readers:
                    if r.eng != eng:
                        self._dep(op, r, True)
        for b in writes:
            if b.last_w is not None:
                self._dep(op, b.last_w, pe_acc and b.last_w.eng == "tensor" and eng == "tensor")
            for r in b.readers:
                self._dep(op, r, eng != "gpsimd")
        for b in reads:
            b.readers.append(op)
        for b in writes:
            b.last_w = op
            b.readers = []
        self.ops[eng].append(op)
        return op

    def dma(self, out, in_, reads=(), writes=(), eng="sync", is_output=False, **kw):
        def fn(e):
            return e.dma_start(out=out, in_=in_, **kw)
        op = Op(eng, len(self.ops[eng]), fn, True)
        for b in reads:
            self._dep(op, b.last_w, False)
        for b in writes:
            if b.last_w is not None and not b.last_w.is_dma:
                self._dep(op, b.last_w, False)
            for r in b.readers:
                self._dep(op, r, False)
        if eng == "gpsimd":
            op.sem, op.val = self.new_sem("sw%d" % self.nsem), 16
            for bb in writes:
                bb.last_w = op
                bb.readers = []
        elif writes:
            b = writes[0]
            if b.wsem is None:
                b.wsem = self.new_sem("w_" + b.name)
            b.wcnt += 16
            op.sem, op.val = b.wsem, b.wcnt
            for bb in writes:
                bb.last_w = op
                bb.readers = []
        else:
            b = reads[0]
            if b.rsem is None:
                b.rsem = self.new_sem("r_" + b.name)
            b.rcnt += 16
            op.sem, op.val = b.rsem, b.rcnt
        for b in reads:
            b.readers.append(op)
        if is_output:
            self.out_dmas.append(op)
        self.dma_latest[id(op.sem)] = op
        self.ops[eng].append(op)
        return op

    def barrier(self):
        lasts = {}
        for x in ENGS:
            for o in reversed(self.ops[x]):
                if not o.is_dma and o.fn is not None:
                    lasts[x] = o
                    break
        dmas = list(self.dma_latest.values())
        for e in ENGS:
            op = Op(e, len(self.ops[e]), None, False)
            for x, l in lasts.items():
                self._dep(op, l, False)
            for d in dmas:
                self._dep(op, d, False)
            self.ops[e].append(op)

    def emit(self):
        nc = self.nc
        fin = Op("sync", len(self.ops["sync"]), None, False)
        seen = {}
        for o in self.out_dmas:
            seen[id(o.sem)] = max(seen.get(id(o.sem), (0, None))[0], o.val), o.sem
        for e in ENGS:
            cnt = 0
            for op in self.ops[e]:
                if op.is_dma or op.fn is None:
                    continue
                if op.signal:
                    cnt += 1
                    op.sem = self.esem[e]
                    op.val = cnt
        ops = self.ops
        finals = list(seen.values())

        with nc.Block() as block:
            def run(e, name):
                for op in ops[name]:
                    for w in op.waits:
                        e.wait_ge(w.sem, w.val)
                    if op.fn is None:
                        continue
                    ins = op.fn(e)
                    if op.is_dma:
                        ins.then_inc(op.sem, 16)
                    elif op.signal:
                        ins.then_inc(op.sem, 1)
                if name == "sync":
                    for val, sem in finals:
                        e.wait_ge(sem, val)

            @block.sync
            def _(e):
                run(e, "sync")

            @block.tensor
            def _(e):
                run(e, "tensor")

            @block.vector
            def _(e):
                run(e, "vector")

            @block.scalar
            def _(e):
                run(e, "scalar")

            @block.gpsimd
            def _(e):
                run(e, "gpsimd")

    def close(self):
        self.stack.close()


import math

D = 1024
S = 8192
NBLK = 64
NG = 8
NOWN = 16
INC = 2720
DFF = 2816
NFC = 22
EPS = 1e-6
PI = math.pi
RET_LOGG = [math.log(1.0 - 2.0 ** (-5.0 - h)) for h in range(4)]
SM_SCALE = 1.0 / math.sqrt(96.0)


class _Stop(Exception):
    pass


_LAST = {}


class Arena:
    def __init__(self, tile, ncols):
        self.t = tile
        self.n = ncols
        self.top = 0

    def f32(self, cols):
        off = self.top
        self.top += cols
        assert self.top <= self.n, ("arena overflow", self.top, self.n)
        return self.t[:, off:off + cols]

    def bf(self, cols):
        c = (cols + 1) // 2
        return self.f32(c).bitcast(BF16)[:, 0:cols]

    def i32(self, cols):
        return self.f32(cols).bitcast(I32)


def build_program(debug=False, ng_run=NG, stop=None):
    nc = bass.Bass("TRN2", target_bir_lowering=False)

    def din(name, shape, dt=F32):
        return nc.dram_tensor(name, shape, dt, kind="ExternalInput").ap()

    x_all = din("x_all", [S, D])
    x_own = din("x_own", [NOWN * 128, D])
    p_own = din("p_own", [NOWN * 128, 256])
    pos_all = din("pos_all", [128, NBLK], I32)
    pos_own = din("pos_own", [128, NOWN], I32)
    wsel = din("wsel", [128, 16])
    amask = din("amask", [128, 8 * 256])
    rope_inv = din("rope_inv", [128, 80])
    pre_mix_norm = din("pre_mix_norm", [D])
    w_in = din("w_in", [D, INC])
    ret_gn_w = din("ret_gn_w", [512])
    mla_q_norm = din("mla_q_norm", [384])
    w_uq = din("w_uq", [384, 768])
    mla_kv_norm = din("mla_kv_norm", [256])
    w_ukv = din("w_ukv", [256, 1024])
    w_o = din("w_o", [D, D])
    post_mix_norm = din("post_mix_norm", [D])
    pre_ffn_norm = din("pre_ffn_norm", [D])
    w_gate = din("w_gate", [D, DFF])
    w_up = din("w_up", [D, DFF])
    w_down = din("w_down", [DFF, D])
    post_ffn_norm = din("post_ffn_norm", [D])
    w_ple_proj = din("w_ple_proj", [256, D])
    ple_norm = din("ple_norm", [D])
    w_ple_gate = din("w_ple_gate", [D, D])
    b_ple_gate = din("b_ple_gate", [D])
    out = nc.dram_tensor("out", [NOWN * 128, D], F32, kind="ExternalOutput").ap()
    dbg = nc.dram_tensor("dbg", [128, 16384], F32, kind="ExternalOutput").ap() if debug else None
    dbg_off = {"n": 0}

    def dump(ap, buf, cols):
        o = dbg_off["n"]
        dbg_off["n"] += cols
        P.dma(dbg[:, o:o + cols], ap, reads=[buf], is_output=True)
        return o

    def finish():
        P.emit()
        P.close()
        return nc

    _LAST["nc"] = nc

    chk_cnt = {}

    def chk(name):
        chk_cnt[name] = chk_cnt.get(name, 0) + 1
        if stop == name or stop == "%s:%d" % (name, chk_cnt[name]):
            dump(stat, statb[0], 64)
            P.emit()
            P.close()
            raise _Stop()
    hs = nc.dram_tensor("hs", [NOWN * 128, D], F32, kind="Internal").ap()

    P = Prog(nc)
    ACOLS = 53184
    A = Arena(P.sbuf("arena", [128, ACOLS], F32)[:, :], ACOLS)

    banks = [(P.psum("ps%d" % i, [128, 512], F32)[:, :], Buf("ps%d" % i, excl=True)) for i in range(8)]
    rot = {"i": 0, "n": 6}

    def bank():
        b = banks[rot["i"] % rot["n"]]
        rot["i"] += 1
        return b

    defer = {"on": False, "list": []}

    def _rec(eng, fn, r, w):
        if defer["on"]:
            defer["list"].append((eng, fn, list(r), list(w)))
            return None
        return P.add(eng, fn, reads=r, writes=w)

    def drip(k):
        for _ in range(min(k, len(defer["list"]))):
            eng, fn, r, w = defer["list"].pop(0)
            P.add(eng, fn, reads=r, writes=w)

    def V(fn, r=(), w=()):
        return _rec("vector", fn, r, w)

    def G(fn, r=(), w=()):
        return _rec("gpsimd", fn, r, w)

    def ACT(fn, r=(), w=()):
        return _rec("scalar", fn, r, w)

    def MM(out_ap, ob, lhsT, rhs, r, start=True, stop=True):
        return P.add("tensor", lambda e: e.matmul(out_ap, lhsT=lhsT, rhs=rhs, start=start, stop=stop),
                     reads=r, writes=[ob], pe_acc=True)

    def TR(out_ap, ob, in_ap, r):
        idn = ident if in_ap.dtype == BF16 else identf
        return P.add("tensor", lambda e: e.transpose(out=out_ap, in_=in_ap, identity=idn[0:in_ap.shape[0], 0:in_ap.shape[0]]),
                     reads=list(r) + [constb], writes=[ob], pe_acc=True)

    if debug:
        P.add("gpsimd", lambda e: e.memset(A.t, 0.0), writes=[])
        P.barrier()
    constb = Buf("const")
    ident = A.bf(128)
    identf = A.f32(128)
    cmask = A.bf(128)
    ones_t = A.f32(128)
    epsc = A.f32(1)
    pidx = A.f32(1)
    dq = A.f32(4)
    dk = A.f32(4)
    gam = A.f32(512)
    inv_r = A.f32(64)
    inv_m = A.f32(16)
    tmpi = A.i32(64)
    stat = A.f32(64)
    statb = [Buf("stat%d" % i) for i in range(8)]
    stat_i = {"i": 0}

    def stats():
        i = stat_i["i"] % 8
        stat_i["i"] += 1
        return stat[:, i * 8:(i + 1) * 8], statb[i]

    G(lambda e: e.memset(ones_t, 1.0), w=[constb])
    G(lambda e: e.memset(epsc, EPS), w=[constb])
    G(lambda e: e.affine_select(out=identf, in_=ones_t, pattern=[[-1, 128]], compare_op=ALU.is_equal, fill=0.0, base=0, channel_multiplier=1), r=[constb], w=[constb])
    G(lambda e: e.tensor_copy(out=ident, in_=identf), r=[constb], w=[constb])
    G(lambda e: e.affine_select(out=cmask, in_=ones_t, pattern=[[1, 128]], compare_op=ALU.is_ge, fill=0.0, base=0, channel_multiplier=-1), r=[constb], w=[constb])
    G(lambda e: e.iota(tmpi[:, 0:1], pattern=[[0, 1]], base=0, channel_multiplier=1), w=[constb])
    G(lambda e: e.tensor_copy(out=pidx, in_=tmpi[:, 0:1]), r=[constb], w=[constb])
    for h in range(4):
        ACT(lambda e, h=h: e.activation(out=dq[:, h:h + 1], in_=pidx, func=AF.Exp, scale=RET_LOGG[h]), r=[constb], w=[constb])
        ACT(lambda e, h=h: e.activation(out=dk[:, h:h + 1], in_=pidx, func=AF.Exp, scale=-RET_LOGG[h]), r=[constb], w=[constb])
        G(lambda e, h=h: e.memset(gam[:, h * 128:(h + 1) * 128], math.exp(128 * RET_LOGG[h])), w=[constb])
    V(lambda e: e.tensor_scalar(out=dk, in0=dk, scalar1=128.0 ** -0.5, scalar2=None, op0=ALU.mult), r=[constb], w=[constb])
    invb = Buf("inv")
    P.dma(inv_r, rope_inv[:, 0:64], writes=[invb])
    P.dma(inv_m, rope_inv[:, 64:80], writes=[invb])

    junk = A.bf(1024)

    def rms_r(srcs, n, rd):
        st, sb = stats()
        for j, (ap, wdt) in enumerate(srcs):
            ACT(lambda e, ap=ap, wdt=wdt, j=j: e.activation(out=junk[:, 0:wdt], in_=ap, func=AF.Square, accum_out=st[:, j:j + 1]),
                r=rd, w=[sb])
        if len(srcs) == 2:
            V(lambda e: e.tensor_tensor(out=st[:, 0:1], in0=st[:, 0:1], in1=st[:, 1:2], op=ALU.add), r=[sb], w=[sb])
        ACT(lambda e: e.activation(out=st[:, 2:3], in_=st[:, 0:1], func=AF.Sqrt, bias=epsc, scale=1.0 / n), r=[sb, constb], w=[sb])
        V(lambda e: e.reciprocal(out=st[:, 3:4], in_=st[:, 2:3]), r=[sb], w=[sb])
        return st[:, 3:4], sb

    def build_tables(posf, nb, half, inv, Cf, Sf, T4, rd, wb, tb):
        n = nb * half
        a3 = lambda t: t[:, 0:n].rearrange("p (b j) -> p b j", b=nb)
        Aa, Bb, Cc, Dd = [t[:, 0:n] for t in T4]
        Bi = Bb.bitcast(I32)
        V(lambda e: e.tensor_tensor(out=a3(T4[0]), in0=posf.unsqueeze(2).to_broadcast([128, nb, half]),
                                    in1=inv.unsqueeze(1).to_broadcast([128, nb, half]), op=ALU.mult), r=list(rd) + [invb], w=[tb])
        V(lambda e: e.tensor_scalar(out=Bi, in0=Aa, scalar1=1.0 / (2 * PI), scalar2=None, op0=ALU.mult), r=[tb], w=[tb])
        V(lambda e: e.tensor_copy(out=Cc, in_=Bi), r=[tb], w=[tb])
        V(lambda e: e.scalar_tensor_tensor(out=Aa, in0=Cc, scalar=-2 * PI, in1=Aa, op0=ALU.mult, op1=ALU.add), r=[tb], w=[tb])
        V(lambda e: e.tensor_scalar(out=Bb, in0=Aa, scalar1=PI, scalar2=-2 * PI, op0=ALU.is_gt, op1=ALU.mult), r=[tb], w=[tb])
        V(lambda e: e.tensor_tensor(out=Cc, in0=Aa, in1=Bb, op=ALU.add), r=[tb], w=[tb])
        V(lambda e: e.tensor_scalar(out=Cc, in0=Cc, scalar1=-PI, scalar2=PI, op0=ALU.max, op1=ALU.min), r=[tb], w=[tb])
        V(lambda e: e.tensor_scalar(out=Dd, in0=Aa, scalar1=PI / 2, scalar2=None, op0=ALU.add), r=[tb], w=[tb])
        V(lambda e: e.tensor_scalar(out=Bb, in0=Dd, scalar1=PI, scalar2=-2 * PI, op0=ALU.is_gt, op1=ALU.mult), r=[tb], w=[tb])
        V(lambda e: e.tensor_tensor(out=Dd, in0=Dd, in1=Bb, op=ALU.add), r=[tb], w=[tb])
        V(lambda e: e.tensor_scalar(out=Dd, in0=Dd, scalar1=-PI, scalar2=PI, op0=ALU.max, op1=ALU.min), r=[tb], w=[tb])
        C3 = Cf.rearrange("p (b j) -> p b j", b=nb)
        S3 = Sf.rearrange("p (b j) -> p b j", b=nb)
        ACT(lambda e: e.activation(out=C3[:, :, 0:half], in_=a3(T4[3]), func=AF.Sin), r=[tb], w=[wb])
        ACT(lambda e: e.activation(out=C3[:, :, half:2 * half], in_=a3(T4[3]), func=AF.Sin), r=[tb], w=[wb])
        ACT(lambda e: e.activation(out=S3[:, :, 0:half], in_=a3(T4[2]), func=AF.Sin, scale=-1.0), r=[tb], w=[wb])
        ACT(lambda e: e.activation(out=S3[:, :, half:2 * half], in_=a3(T4[2]), func=AF.Sin), r=[tb], w=[wb])

    def rope_tm(eng, src3, Cb, Sb, nh, hd, t1, t2, out3, rd, tb, wb):
        half = hd // 2
        t13 = t1.rearrange("p (h d) -> p h d", h=nh)
        t23 = t2.rearrange("p (h d) -> p h d", h=nh)
        add = lambda fn, r, w: P.add(eng, fn, reads=r, writes=w)
        tbs = tb if isinstance(tb, list) else [tb]
        add(lambda e: e.tensor_tensor(out=t13, in0=src3, in1=Cb.unsqueeze(1).to_broadcast([128, nh, hd]), op=ALU.mult), list(rd), tbs)
        add(lambda e: e.tensor_tensor(out=t23[:, :, 0:half], in0=src3[:, :, half:hd], in1=Sb[:, 0:half].unsqueeze(1).to_broadcast([128, nh, half]), op=ALU.mult), list(rd), tbs)
        add(lambda e: e.tensor_tensor(out=t23[:, :, half:hd], in0=src3[:, :, 0:half], in1=Sb[:, half:hd].unsqueeze(1).to_broadcast([128, nh, half]), op=ALU.mult), list(rd), tbs)
        add(lambda e: e.tensor_tensor(out=out3, in0=t13, in1=t23, op=ALU.add), tbs, [wb])

    cast_i = {"n": 0}

    def staged_load(dst3, dbufs, src3, nk, ncols, stages, chunk_cols, scale_col=None, scale_buf=None, queue="sync"):
        for k in range(nk):
            c = 0
            while c < ncols:
                w = min(chunk_cols, ncols - c)
                sa, sb_ = stages[cast_i["n"] % len(stages)]
                q_ = queue if queue != "alt" else ("sync" if cast_i["n"] % 2 == 0 else "scalar")
                P.dma(sa[:, 0:w], src3[:, k, c:c + w], writes=[sb_], eng=q_)
                use_act = (cast_i["n"] % 2 == 0)
                cast_i["n"] += 1
                dst = dst3[:, k, c:c + w]
                src = sa[:, 0:w]
                rd = [sb_] + ([scale_buf] if scale_buf is not None else [])
                if scale_col is not None:
                    sc = scale_col[:, k:k + 1]
                    if use_act:
                        ACT(lambda e, dst=dst, src=src, sc=sc: e.activation(out=dst, in_=src, func=AF.Copy, scale=sc), r=rd, w=[dbufs[k]])
                    else:
                        V(lambda e, dst=dst, src=src, sc=sc: e.tensor_scalar(out=dst, in0=src, scalar1=sc, scalar2=None, op0=ALU.mult), r=rd, w=[dbufs[k]])
                else:
                    if use_act:
                        ACT(lambda e, dst=dst, src=src: e.copy(out=dst, in_=src), r=rd, w=[dbufs[k]])
                    else:
                        V(lambda e, dst=dst, src=src: e.tensor_copy(out=dst, in_=src), r=rd, w=[dbufs[k]])
                c += w

    m_carry = A.top
    ckvnT = A.bf(2 * S)
    ckvnTb = Buf("ckvnT")
    KT_A = A.bf(S)
    KT_Ab = Buf("KT_A")
    cqnT = A.bf(3 * 2048)
    cqnTb = Buf("cqnT")
    retT = A.bf(4 * 2048)
    retTb = Buf("retT")
    m_ph12 = A.top

    w_in_sb = A.bf(8 * INC)
    w_in3 = w_in_sb.rearrange("p (k n) -> p k n", k=8)
    pmn = A.f32(8)
    XS = [(A.f32(1024), Buf("xs%d" % i)) for i in range(3)]
    XNB = [(A.bf(1024), Buf("xnb%d" % i)) for i in range(2)]
    XNT = [(A.bf(1024), Buf("xnT%d" % i)) for i in range(3)]
    kd_ = A.f32(512); kdb_ = Buf("kd")
    t1_ = A.f32(512); t2_ = A.f32(512); ropeb_ = Buf("ropet")
    KB = [(A.bf(512), Buf("kb%d" % i)) for i in range(2)]
    VB = [(A.bf(512), Buf("vb%d" % i)) for i in range(2)]
    qb = A.bf(512); qbb = Buf("qb")
    qT = A.bf(512); qTb = Buf("qT")
    kT = A.bf(512); kTb = Buf("kT")
    PT = A.bf(512); PTb = Buf("PT")
    yn = A.f32(512); ynb = Buf("yn")
    sg = A.f32(512); sgb = Buf("sg")
    ro = A.bf(512); rob = Buf("ro")
    NTB = 10
    TABS = []
    for _i in range(2):
        TABS.append({"Cfr": A.f32(NTB * 128), "Sfr": A.f32(NTB * 128), "tabb": Buf("tab_r%d" % _i),
                     "Cfm": A.f32(8 * 32), "Sfm": A.f32(8 * 32), "tabmb": Buf("tab_m%d" % _i)})
    T4all = A.f32(4 * 5 * 64)
    T4 = [T4all[:, i * 320:(i + 1) * 320] for i in range(4)]; t4b = Buf("t4")
    posg = A.f32(NTB); posgb = Buf("posg")
    pos_all_i = A.i32(NBLK); pos_own_i = A.i32(NOWN)
    pos_all_f = A.f32(NBLK); pos_own_f = A.f32(NOWN); posb = Buf("pos")
    wsel_sb = A.f32(16); wselb = Buf("wsel")
    Rst = A.f32(512); Rb_ = Buf("R")
    Tt = A.f32(512); Ttb = Buf("T")
    Rlo = A.f32(512); Rhi = A.f32(512); Rselb = Buf("Rlohi")
    Rsel_bf = A.bf(1024); Rselbfb = Buf("Rsel_bf")
    ckvn = A.bf(256); ckvnb = Buf("ckvn")
    krst = A.bf(96); krstb = Buf("krst")
    krt = A.f32(64); krtb = Buf("krt")
    cqn = A.bf(384); cqnb = Buf("cqn")
    gnst = A.f32(32); gnstb = Buf("gnst")

    pmnb = Buf("pmn")
    P.dma(pmn, pre_mix_norm.rearrange("(k p) -> p k", p=128), writes=[pmnb], allow_slow_non_contiguous=True)
    ckv_f32 = ckvnT.bitcast(F32)
    stg0 = [(ckv_f32[:, i * INC:(i + 1) * INC], Buf("stg0_%d" % i)) for i in range(3)]
    w_inb = [Buf("w_in%d" % k) for k in range(8)]
    staged_load(w_in3, w_inb, w_in.rearrange("(k p) n -> p k n", p=128), 8, INC, stg0, INC, scale_col=pmn, scale_buf=pmnb)
    P.dma(pos_all_i, pos_all, writes=[posb])
    P.dma(pos_own_i, pos_own, writes=[posb])
    P.dma(wsel_sb, wsel, writes=[wselb])
    V(lambda e: e.tensor_copy(out=pos_all_f, in_=pos_all_i), r=[posb], w=[posb])
    V(lambda e: e.tensor_copy(out=pos_own_f, in_=pos_own_i), r=[posb], w=[posb])
    G(lambda e: e.memset(krst, 0.0), w=[krstb])
    G(lambda e: e.memset(Rst, 0.0), w=[Rb_])

    if stop == "ph0":
        dump(w_in_sb.bitcast(F32)[:, 0:512], w_inb[0], 512)
        dump(dk, constb, 4)
        dump(inv_r, invb, 64)
        dump(pos_all_f, posb, 64)
        return finish()
    xload = {"n": 0}

    def load_x(src_rows):
        xs, xb = XS[xload["n"] % 3]
        xload["n"] += 1
        P.dma(xs, src_rows, writes=[xb])
        return xs, xb

    def norm_scale(xs, xb, n):
        r, rb = rms_r([(xs, 1024)], 1024.0, [xb])
        xn, xnb_ = XNB[n % 2]
        V(lambda e: e.tensor_scalar(out=xn, in0=xs, scalar1=r, scalar2=None, op0=ALU.mult), r=[xb, rb], w=[xnb_])

    def transpose_x(n, bk):
        xn, xnb_ = XNB[n % 2]
        ps, pb = bk
        psb = ps.bitcast(BF16)
        for kc in range(8):
            TR(psb[:, kc * 128:(kc + 1) * 128], pb, xn[:, kc * 128:(kc + 1) * 128], [xnb_])
        xt, xtb = XNT[n % 3]
        ACT(lambda e: e.copy(out=xt, in_=psb), r=[pb], w=[xtb])
        return xt.rearrange("p (k t) -> p k t", k=8), xtb

    def proj(xt3, xtb, c0, ncols, bk):
        ps, pb = bk
        for kc in range(8):
            MM(ps[:, 0:ncols], pb, xt3[:, kc, :], w_in3[:, kc, c0:c0 + ncols], [xtb, w_inb[kc]], start=(kc == 0), stop=(kc == 7))
        return ps, pb

    def ret_rope(ps, pb, dscale, tidx, out_bf, outb, eng, g, alt=False):
        Cfr, Sfr, tabb = TABS[g % 2]["Cfr"], TABS[g % 2]["Sfr"], TABS[g % 2]["tabb"]
        if alt:
            kd, kdb, t1, t2, ropeb = T4all[:, 0:512], t4b, T4all[:, 512:1024], yn, t4b
            ropew = [t4b, ynb]
        else:
            kd, kdb, t1, t2, ropeb = kd_, kdb_, t1_, t2_, ropeb_
            ropew = [ropeb_]
        for h in range(4):
            ACT(lambda e, h=h: e.activation(out=kd[:, h * 128:(h + 1) * 128], in_=ps[:, h * 128:(h + 1) * 128], func=AF.Copy, scale=dscale[:, h:h + 1]),
                r=[pb, constb], w=[kdb])
        kd3 = kd.rearrange("p (h d) -> p h d", h=4)
        rope_tm(eng, kd3, Cfr[:, tidx * 128:(tidx + 1) * 128], Sfr[:, tidx * 128:(tidx + 1) * 128], 4, 128, t1, t2,
                out_bf.rearrange("p (h d) -> p h d", h=4), [kdb, tabb], ropew, outb)

    seq = []
    for g in range(ng_run):
        for i in range(8):
            seq.append(("all", g * 8 + i, g, i))
        seq.append(("own", 2 * g, g, 0))
        seq.append(("own", 2 * g + 1, g, 1))
    NSEQ = len(seq)

    def src_rows(k):
        kind, idx = seq[k][0], seq[k][1]
        return (x_all if kind == "all" else x_own)[idx * 128:(idx + 1) * 128, :]

    loaded = {}
    for k in range(min(2, NSEQ)):
        loaded[k] = load_x(src_rows(k))
    st = {}

    def S1(n):
        xs, xb = loaded.pop(n)
        if n + 2 < NSEQ:
            loaded[n + 2] = load_x(src_rows(n + 2))
        norm_scale(xs, xb, n)

    def S2(n):
        B4 = banks[4 * (n % 2):4 * (n % 2) + 4]
        xt3, xtb = transpose_x(n, B4[0])
        st[n] = {"xt3": xt3, "xtb": xtb, "B4": B4}

    def S3(n):
        kind, idx, g, i = seq[n]
        d = st[n]
        xt3, xtb, B4 = d["xt3"], d["xtb"], d["B4"]
        if kind == "all":
            d["rk"] = proj(xt3, xtb, 512, 512, B4[1])
            d["rv"] = proj(xt3, xtb, 1024, 512, B4[2])
            d["ck"] = proj(xt3, xtb, 2432, 288, B4[3])
        else:
            d["rq"] = proj(xt3, xtb, 0, 512, B4[1])
            d["rk"] = proj(xt3, xtb, 512, 512, B4[2])
            d["rv"] = proj(xt3, xtb, 1024, 512, B4[3])

    def tables(g):
        V(lambda e: e.tensor_copy(out=posg[:, 0:8], in_=pos_all_f[:, g * 8:(g + 1) * 8]), r=[posb], w=[posgb])
        V(lambda e: e.tensor_copy(out=posg[:, 8:10], in_=pos_own_f[:, 2 * g:2 * g + 2]), r=[posb], w=[posgb])
        tb_ = TABS[g % 2]
        build_tables(posg[:, 0:5], 5, 64, inv_r, tb_["Cfr"][:, 0:640], tb_["Sfr"][:, 0:640], T4, [posgb], tb_["tabb"], t4b)
        build_tables(posg[:, 5:10], 5, 64, inv_r, tb_["Cfr"][:, 640:1280], tb_["Sfr"][:, 640:1280], T4, [posgb], tb_["tabb"], t4b)
        build_tables(posg[:, 0:8], 8, 16, inv_m, tb_["Cfm"], tb_["Sfm"], T4, [posgb], tb_["tabmb"], t4b)

    def B_elem_all(n):
        kind, blk, g, i = seq[n]
        d = st[n]
        B4 = d["B4"]
        rk_ps, rk_b = d["rk"]; rv_ps, rv_b = d["rv"]; ck_ps, ck_b = d["ck"]
        kb_, kbb = KB[n % 2]
        vb_, vbb = VB[n % 2]
        r, rb = rms_r([(ck_ps[:, 0:256], 256)], 256.0, [ck_b])
        ret_rope(rk_ps, rk_b, dk, i, kb_, kbb, "gpsimd", g)
        ACT(lambda e: e.copy(out=vb_, in_=rv_ps), r=[rv_b], w=[vbb])
        V(lambda e: e.tensor_scalar(out=ckvn, in0=ck_ps[:, 0:256], scalar1=r, scalar2=None, op0=ALU.mult), r=[ck_b, rb], w=[ckvnb])
        V(lambda e: e.tensor_copy(out=krt[:, 0:32], in_=ck_ps[:, 256:288]), r=[ck_b], w=[krtb])
        Cfm, Sfm, tabmb = TABS[g % 2]["Cfm"], TABS[g % 2]["Sfm"], TABS[g % 2]["tabmb"]
        rope_tm("vector", krt[:, 0:32].rearrange("p (h d) -> p h d", h=1), Cfm[:, i * 32:(i + 1) * 32], Sfm[:, i * 32:(i + 1) * 32], 1, 32,
                krt[:, 32:64], gnst[:, 0:32], krst[:, 64:96].rearrange("p (h d) -> p h d", h=1), [krtb, tabmb], krtb, krstb)

    def B_state(n):
        kind, blk, g, i = seq[n]
        if kind != "all":
            return
        u_ps, u_b = ubank.pop(n)
        G(lambda e: e.tensor_tensor(out=Tt, in0=Rst, in1=gam, op=ALU.mult), r=[Rb_, constb], w=[Ttb])
        if i == 0:
            V(lambda e: e.tensor_scalar(out=Rlo, in0=Tt, scalar1=wsel_sb[:, 0:1], scalar2=None, op0=ALU.mult), r=[Ttb, wselb], w=[Rselb])
            V(lambda e: e.tensor_scalar(out=Rhi, in0=Tt, scalar1=wsel_sb[:, 8:9], scalar2=None, op0=ALU.mult), r=[Ttb, wselb], w=[Rselb])
        else:
            V(lambda e: e.scalar_tensor_tensor(out=Rlo, in0=Tt, scalar=wsel_sb[:, i:i + 1], in1=Rlo, op0=ALU.mult, op1=ALU.add), r=[Ttb, wselb, Rselb], w=[Rselb])
            V(lambda e: e.scalar_tensor_tensor(out=Rhi, in0=Tt, scalar=wsel_sb[:, 8 + i:9 + i], in1=Rhi, op0=ALU.mult, op1=ALU.add), r=[Ttb, wselb, Rselb], w=[Rselb])
        V(lambda e: e.tensor_tensor(out=Rst, in0=Tt, in1=u_ps, op=ALU.add), r=[Ttb, u_b], w=[Rb_])
        if i == 7:
            V(lambda e: e.tensor_copy(out=Rsel_bf[:, 0:512], in_=Rlo), r=[Rselb], w=[Rselbfb])
            V(lambda e: e.tensor_copy(out=Rsel_bf[:, 512:1024], in_=Rhi), r=[Rselb], w=[Rselbfb])

    ubank = {}

    def B_pe_all(n):
        kind, blk, g, i = seq[n]
        d = st.pop(n)
        B4 = d["B4"]
        kb_, kbb = KB[n % 2]
        vb_, vbb = VB[n % 2]
        u_ps, u_b = B4[2]
        for h in range(4):
            MM(u_ps[:, h * 128:(h + 1) * 128], u_b, kb_[:, h * 128:(h + 1) * 128], vb_[:, h * 128:(h + 1) * 128], [kbb, vbb])
        ubank[n] = (u_ps, u_b)
        tp, tpb = B4[0]
        tpbf = tp.bitcast(BF16)
        TR(tpbf[:, 0:128], tpb, ckvn[:, 0:128], [ckvnb])
        TR(tpbf[:, 128:256], tpb, ckvn[:, 128:256], [ckvnb])
        ck3_ = ckvnT.rearrange("p (k t) -> p k t", k=2)
        TR(tpbf[0:96, 256:384], tpb, krst, [krstb])
        V(lambda e: e.tensor_copy(out=ck3_[:, :, blk * 128:(blk + 1) * 128], in_=tpbf[:, 0:256].rearrange("p (k t) -> p k t", k=2)),
          r=[tpb], w=[ckvnTb])
        V(lambda e: e.tensor_copy(out=KT_A[64:96, blk * 128:(blk + 1) * 128], in_=tpbf[64:96, 256:384]), r=[tpb], w=[KT_Ab])

    def B_elem_own(n):
        kind, ib, g, s_ = seq[n]
        d = st[n]
        rq_ps, rq_b = d["rq"]; rk_ps, rk_b = d["rk"]; rv_ps, rv_b = d["rv"]
        kb_, kbb = KB[n % 2]
        vb_, vbb = VB[n % 2]
        ret_rope(rq_ps, rq_b, dq, 8 + s_, qb, qbb, "vector", g, alt=True)
        ret_rope(rk_ps, rk_b, dk, 8 + s_, kb_, kbb, "gpsimd", g)
        ACT(lambda e: e.copy(out=vb_, in_=rv_ps), r=[rv_b], w=[vbb])

    def B_pe_own(n):
        kind, ib, g, s_ = seq[n]
        d = st.pop(n)
        B4 = d["B4"]
        xt3, xtb = d["xt3"], d["xtb"]
        kb_, kbb = KB[n % 2]
        vb_, vbb = VB[n % 2]
        rg_ps, rg_b = proj(xt3, xtb, 1536, 512, B4[0])
        cq_ps, cq_b = proj(xt3, xtb, 2048, 384, B4[1])
        ACT(lambda e: e.activation(out=sg, in_=rg_ps, func=AF.Silu), r=[rg_b], w=[sgb])
        r, rb = rms_r([(cq_ps[:, 0:384], 384)], 384.0, [cq_b])
        V(lambda e: e.tensor_scalar(out=cqn, in0=cq_ps[:, 0:384], scalar1=r, scalar2=None, op0=ALU.mult), r=[cq_b, rb], w=[cqnb])
        tq, tqb = B4[2]
        tqbf = tq.bitcast(BF16)
        for h in range(4):
            TR(tqbf[:, h * 128:(h + 1) * 128], tqb, qb[:, h * 128:(h + 1) * 128], [qbb])
        for h in range(4):
            TR(tqbf[:, 512 + h * 128:512 + (h + 1) * 128], tqb, kb_[:, h * 128:(h + 1) * 128], [kbb])
        ACT(lambda e: e.copy(out=qT, in_=tqbf[:, 0:512]), r=[tqb], w=[qTb])
        ACT(lambda e: e.copy(out=kT, in_=tqbf[:, 512:1024]), r=[tqb], w=[kTb])
        tp, tpb = B4[3]
        tpbf = tp.bitcast(BF16)
        for kc in range(3):
            TR(tpbf[:, kc * 128:(kc + 1) * 128], tpb, cqn[:, kc * 128:(kc + 1) * 128], [cqnb])
        cq3_ = cqnT.rearrange("p (k t) -> p k t", k=3)
        V(lambda e: e.tensor_copy(out=cq3_[:, :, ib * 128:(ib + 1) * 128], in_=tpbf[:, 0:384].rearrange("p (k t) -> p k t", k=3)),
          r=[tpb], w=[cqnTb])
        s_ps, s_b = B4[0]
        for h in range(4):
            MM(s_ps[:, h * 128:(h + 1) * 128], s_b, kT[:, h * 128:(h + 1) * 128], qT[:, h * 128:(h + 1) * 128], [kTb, qTb])
        V(lambda e: e.tensor_tensor(out=PT.rearrange("p (h c) -> p h c", h=4), in0=s_ps.rearrange("p (h c) -> p h c", h=4),
                                    in1=cmask.unsqueeze(1).to_broadcast([128, 4, 128]), op=ALU.mult), r=[s_b, constb], w=[PTb])
        y_ps, y_b = B4[1]
        for h in range(4):
            MM(y_ps[:, h * 128:(h + 1) * 128], y_b, PT[:, h * 128:(h + 1) * 128], vb_[:, h * 128:(h + 1) * 128], [PTb, vbb], start=True, stop=False)
            MM(y_ps[:, h * 128:(h + 1) * 128], y_b, qT[:, h * 128:(h + 1) * 128], Rsel_bf[:, s_ * 512 + h * 128:s_ * 512 + (h + 1) * 128], [qTb, Rselbfb], start=False, stop=True)
        for h in range(4):
            V(lambda e, h=h: e.bn_stats(out=gnst[:, h * 6:(h + 1) * 6], in_=y_ps[:, h * 128:(h + 1) * 128]), r=[y_b], w=[gnstb])
        for h in range(4):
            V(lambda e, h=h: e.bn_aggr(out=gnst[:, 24 + 2 * h:26 + 2 * h], in_=gnst[:, h * 6:(h + 1) * 6]), r=[gnstb], w=[gnstb])
        mv = gnst[:, 24:32].rearrange("p (h t) -> p h t", h=4)
        ACT(lambda e: e.activation(out=gnst[:, 0:4], in_=mv[:, :, 1], func=AF.Sqrt, bias=epsc, scale=1.0), r=[gnstb, constb], w=[gnstb])
        V(lambda e: e.reciprocal(out=gnst[:, 4:8], in_=gnst[:, 0:4]), r=[gnstb], w=[gnstb])
        for h in range(4):
            V(lambda e, h=h: e.tensor_scalar(out=yn[:, h * 128:(h + 1) * 128], in0=y_ps[:, h * 128:(h + 1) * 128],
                                             scalar1=gnst[:, 24 + 2 * h:25 + 2 * h], scalar2=gnst[:, 4 + h:5 + h], op0=ALU.subtract, op1=ALU.mult),
              r=[y_b, gnstb], w=[ynb])
        G(lambda e: e.tensor_tensor(out=ro, in0=yn, in1=sg, op=ALU.mult), r=[ynb, sgb], w=[rob])
        tr_, trb = B4[2]
        trbf = tr_.bitcast(BF16)
        for h in range(4):
            TR(trbf[:, h * 128:(h + 1) * 128], trb, ro[:, h * 128:(h + 1) * 128], [rob])
        rt3_ = retT.rearrange("p (k t) -> p k t", k=4)
        ACT(lambda e: e.copy(out=rt3_[:, :, ib * 128:(ib + 1) * 128], in_=trbf[:, 0:512].rearrange("p (k t) -> p k t", k=4)),
            r=[trb], w=[retTb])

    def B_elem(n):
        kind, idx, g, i = seq[n]
        if kind == "all" and i == 0:
            drip(10 ** 6)
            if g + 1 < ng_run:
                defer["on"] = True
                tables(g + 1)
                defer["on"] = False
        (B_elem_all if kind == "all" else B_elem_own)(n)
        if kind == "all":
            drip(7)

    def B_pe(n):
        (B_pe_all if seq[n][0] == "all" else B_pe_own)(n)

    if NSEQ:
        tables(0)
    for fn, k in ((S1, 0), (S1, 1), (S2, 0), (S1, 2), (S2, 1), (S3, 0)):
        if k < NSEQ:
            fn(k)
    for n in range(NSEQ):
        if n > 0:
            B_state(n - 1)
        if n + 3 < NSEQ:
            S1(n + 3)
        if n + 2 < NSEQ:
            S2(n + 2)
        B_elem(n)
        if n + 1 < NSEQ:
            S3(n + 1)
        B_pe(n)

    if stop == "ph12":
        dump(ckvnT.bitcast(F32)[:, 0:512], ckvnTb, 512)
        dump(KT_A.bitcast(F32)[:, 0:512], KT_Ab, 512)
        dump(retT.bitcast(F32)[:, 0:128], retTb, 128)
        dump(cqnT.bitcast(F32)[:, 0:128], cqnTb, 128)
        dump(Rst, Rb_, 512)
        dump(Rsel_bf.bitcast(F32), Rselbfb, 512)
        dump(TABS[0]["Cfr"], TABS[0]["tabb"], NTB * 128)
        dump(TABS[0]["Sfr"], TABS[0]["tabb"], NTB * 128)
        dump(KB[1][0].bitcast(F32), KB[1][1], 256)
        dump(VB[1][0].bitcast(F32), VB[1][1], 256)
        dump(qb.bitcast(F32), qbb, 256)
        dump(yn, ynb, 512)
        dump(sg, sgb, 512)
        return finish()
    P.barrier()
    A.top = m_ph12
    mlaT = A.bf(4 * 2048); mlaTb = Buf("mlaT")
    mlaT3 = mlaT.rearrange("p (k t) -> p k t", k=4)
    m_4a = A.top
    KT_B = A.bf(S); KT_Bb = Buf("KT_B")
    Vp = A.bf(NBLK * 2 * 66); Vpb = Buf("Vp")
    Vp4 = Vp.rearrange("p (b a d) -> p b a d", b=NBLK, a=2)
    w_ukv_sb = A.bf(2 * 1024); w_ukvb = Buf("w_ukv")
    w_ukv3 = w_ukv_sb.rearrange("p (k n) -> p k n", k=2)
    w_uq_sb = A.bf(3 * 768); w_uqb = Buf("w_uq")
    w_uq3 = w_uq_sb.rearrange("p (k n) -> p k n", k=3)
    w_rot = A.bf(3 * 768); w_rotb = Buf("w_rot")
    w_rot3 = w_rot.rearrange("p (k n) -> p k n", k=3)
    kvn = A.f32(2); qn = A.f32(3)
    QT = A.bf(2 * 2048); QTb = Buf("QT")
    QT3 = QT.rearrange("p (a t) -> p a t", a=2)
    cosT = A.f32(2048); sinT = A.f32(2048); csTb = Buf("csT")
    am = A.bf(8 * 256); amb = Buf("am")
    am3 = am.rearrange("p (i q) -> p i q", i=8)
    PTS = [(A.bf(512), Buf("PTa%d" % i)) for i in range(4)]
    onrm = A.bf(256); onrmb = Buf("onrm")
    o_sb = A.f32(256); o_sbb = Buf("o_sb")
    rden = A.f32(256); rdenb = Buf("rden")
    Cq = A.f32(NOWN * 32); Sq = A.f32(NOWN * 32); tabqb = Buf("tabq")
    padc = A.f32(96); pads = A.f32(96); padb = Buf("pad")
    T4q = [A.f32(NOWN * 16) for _ in range(4)]; t4qb = Buf("t4q")
    m_ph3 = A.top

    P.dma(w_ukv3, w_ukv.rearrange("(k p) n -> p k n", p=128), writes=[w_ukvb], eng="gpsimd")
    P.dma(w_uq3, w_uq.rearrange("(k p) n -> p k n", p=128), writes=[w_uqb], eng="gpsimd")
    kvnb = Buf("kvn")
    P.dma(kvn, mla_kv_norm.rearrange("(k p) -> p k", p=128), writes=[kvnb], allow_slow_non_contiguous=True)
    P.dma(qn, mla_q_norm.rearrange("(k p) -> p k", p=128), writes=[kvnb], allow_slow_non_contiguous=True)
    P.dma(am, amask, writes=[amb], eng="gpsimd")
    V(lambda e: e.tensor_scalar(out=am, in0=am, scalar1=1.0, scalar2=30000.0, op0=ALU.subtract, op1=ALU.mult), r=[amb], w=[amb])
    for kc in range(2):
        V(lambda e, kc=kc: e.tensor_scalar(out=w_ukv3[:, kc, :], in0=w_ukv3[:, kc, :], scalar1=kvn[:, kc:kc + 1], scalar2=None, op0=ALU.mult), r=[w_ukvb, kvnb], w=[w_ukvb])
    for kc in range(3):
        V(lambda e, kc=kc: e.tensor_scalar(out=w_uq3[:, kc, :], in0=w_uq3[:, kc, :], scalar1=qn[:, kc:kc + 1], scalar2=None, op0=ALU.mult), r=[w_uqb, kvnb], w=[w_uqb])
    G(lambda e: e.memset(w_rot, 0.0), w=[w_rotb])
    wq4 = w_uq_sb.rearrange("p (k h d) -> p k h d", k=3, h=8)
    wr4 = w_rot.rearrange("p (k h d) -> p k h d", k=3, h=8)
    for kc in range(3):
        V(lambda e, kc=kc: e.tensor_copy(out=wr4[:, kc, :, 64:80], in_=wq4[:, kc, :, 80:96]), r=[w_uqb, w_rotb], w=[w_rotb])
        V(lambda e, kc=kc: e.tensor_copy(out=wr4[:, kc, :, 80:96], in_=wq4[:, kc, :, 64:80]), r=[w_uqb, w_rotb], w=[w_rotb])
    G(lambda e: e.memset(Vp4[:, :, :, 64:65], 1.0), w=[Vpb])
    V(lambda e: e.tensor_copy(out=KT_B[64:96, :], in_=KT_A[64:96, :]), r=[KT_Ab], w=[KT_Bb])
    build_tables(pos_own_f[:, 0:NOWN], NOWN, 16, inv_m, Cq, Sq, T4q, [posb], tabqb, t4qb)
    G(lambda e: e.memset(padc, 0.0), w=[padb])
    G(lambda e: e.memset(pads, 0.0), w=[padb])
    for ib in range(NOWN):
        V(lambda e, ib=ib: e.tensor_scalar(out=padc[:, 64:96], in0=Cq[:, ib * 32:(ib + 1) * 32], scalar1=SM_SCALE, scalar2=None, op0=ALU.mult), r=[tabqb], w=[padb])
        V(lambda e, ib=ib: e.tensor_scalar(out=pads[:, 64:96], in0=Sq[:, ib * 32:(ib + 1) * 32], scalar1=SM_SCALE, scalar2=None, op0=ALU.mult), r=[tabqb], w=[padb])
        tp, tpb = bank()
        TR(tp[0:96, 0:128], tpb, padc, [padb])
        TR(tp[0:96, 128:256], tpb, pads, [padb])
        V(lambda e, tp=tp, ib=ib: e.tensor_copy(out=cosT[64:96, ib * 128:(ib + 1) * 128], in_=tp[64:96, 0:128]), r=[tpb], w=[csTb])
        V(lambda e, tp=tp, ib=ib: e.tensor_copy(out=sinT[64:96, ib * 128:(ib + 1) * 128], in_=tp[64:96, 128:256]), r=[tpb], w=[csTb])

    ck3 = ckvnT.rearrange("p (k t) -> p k t", k=2)
    cq3 = cqnT.rearrange("p (k t) -> p k t", k=3)
    wkv4 = w_ukv_sb.rearrange("p (k h t d) -> p k h t d", k=2, h=8, t=2)
    u1 = A.f32(512); u2 = A.f32(512); ub = Buf("u12")
    evac = {"n": 0}

    def evac_copy(out_ap, in_ap, r, w, scale=None):
        k = evac["n"] % 2
        evac["n"] += 1
        if scale is not None:
            ACT(lambda e: e.activation(out=out_ap, in_=in_ap, func=AF.Copy, scale=scale), r=r, w=w)
        elif k == 0:
            ACT(lambda e: e.copy(out=out_ap, in_=in_ap), r=r, w=w)
        else:
            V(lambda e: e.tensor_copy(out=out_ap, in_=in_ap), r=r, w=w)

    acc_i = {"n": 0}
    for hp in range(4):
        KTs = [(KT_A, KT_Ab), (KT_B, KT_Bb)]
        for a in range(2):
            h = 2 * hp + a
            KT, KTb_ = KTs[a]
            for tt in range(16):
                ps, pb = bank()
                for kc in range(2):
                    MM(ps[0:64, :], pb, w_ukv3[:, kc, h * 128:h * 128 + 64], ck3[:, kc, tt * 512:(tt + 1) * 512], [w_ukvb, ckvnTb], start=(kc == 0), stop=(kc == 1))
                evac_copy(KT[0:64, tt * 512:(tt + 1) * 512], ps[0:64, :], [pb], [KTb_])
        for b4 in range(16):
            ps, pb = bank()
            for j in range(4):
                blk = b4 * 4 + j
                for kc in range(2):
                    MM(ps[:, j * 128:(j + 1) * 128], pb, ck3[:, kc, blk * 128:(blk + 1) * 128], wkv4[:, kc, 2 * hp:2 * hp + 2, 1, :], [w_ukvb, ckvnTb], start=(kc == 0), stop=(kc == 1))
            evac_copy(Vp4[:, b4 * 4:(b4 + 1) * 4, :, 0:64], ps.rearrange("p (j a d) -> p j a d", j=4, a=2), [pb], [Vpb])
        for a in range(2):
            h = 2 * hp + a
            for tt in range(4):
                p1, p1b = bank()
                p2, p2b = bank()
                for kc in range(3):
                    MM(p1[0:96, :], p1b, w_uq3[:, kc, h * 96:(h + 1) * 96], cq3[:, kc, tt * 512:(tt + 1) * 512], [w_uqb, cqnTb], start=(kc == 0), stop=(kc == 2))
                for kc in range(3):
                    MM(p2[0:96, :], p2b, w_rot3[:, kc, h * 96:(h + 1) * 96], cq3[:, kc, tt * 512:(tt + 1) * 512], [w_rotb, cqnTb], start=(kc == 0), stop=(kc == 2))
                ACT(lambda e, p1=p1, a=a, tt=tt: e.activation(out=QT3[0:64, a, tt * 512:(tt + 1) * 512], in_=p1[0:64, :], func=AF.Copy, scale=SM_SCALE), r=[p1b], w=[QTb])
                V(lambda e, p1=p1, tt=tt: e.tensor_tensor(out=u1[64:96, :], in0=p1[64:96, :], in1=cosT[64:96, tt * 512:(tt + 1) * 512], op=ALU.mult), r=[p1b, csTb], w=[ub])
                V(lambda e, p2=p2, tt=tt: e.tensor_tensor(out=u2[64:96, :], in0=p2[64:96, :], in1=sinT[64:96, tt * 512:(tt + 1) * 512], op=ALU.mult), r=[p2b, csTb], w=[ub])
                V(lambda e, a=a, tt=tt: e.tensor_tensor(out=QT3[64:96, a, tt * 512:(tt + 1) * 512], in0=u1[64:96, :], in1=u2[64:96, :], op=ALU.add), r=[ub], w=[QTb])
        steps = []
        for g in range(NG):
            nkb = 8 * (g + 1)
            for a in range(2):
                acc, accb = banks[6 + (acc_i["n"] % 2)]
                acc_i["n"] += 1
                for kp in range(nkb // 2):
                    steps.append((g, a, kp, nkb, acc, accb))
        LOOK = 2
        sbanks = {}

        def emit_S(t):
            g, a, kp, nkb, acc, accb = steps[t]
            KT, KTb_ = KTs[a]
            s_ps, s_b = bank()
            sbanks[t] = (s_ps, s_b)
            for j in range(2):
                blk = 2 * kp + j
                diag = blk >= 8 * g
                MM(s_ps[:, j * 256:(j + 1) * 256], s_b, KT[0:96, blk * 128:(blk + 1) * 128], QT3[0:96, a, g * 256:(g + 1) * 256], [KTb_, QTb],
                   start=True, stop=not diag)
                if diag:
                    MM(s_ps[:, j * 256:(j + 1) * 256], s_b, ident, am3[:, blk - 8 * g, :], [constb, amb], start=False, stop=True)

        def emit_F2(g, a):
            bc, bcb = bank()
            MM(bc[0:64, 0:256], bcb, ones_t[64:65, 0:64], rden[64:65, :], [constb, rdenb])
            dst = mlaT3[a * 64:(a + 1) * 64, hp, g * 256:(g + 1) * 256]
            if a == 0:
                V(lambda e: e.tensor_tensor(out=dst, in0=o_sb[0:64, :], in1=bc[0:64, 0:256], op=ALU.mult),
                  r=[o_sbb, bcb], w=[mlaTb])
            else:
                V(lambda e: e.tensor_tensor(out=onrm[0:64, :], in0=o_sb[0:64, :], in1=bc[0:64, 0:256], op=ALU.mult), r=[o_sbb, bcb], w=[onrmb])
                V(lambda e: e.tensor_copy(out=dst, in_=onrm[0:64, :]), r=[onrmb], w=[mlaTb])

        def emit_rest(t):
            g, a, kp, nkb, acc, accb = steps[t]
            s_ps, s_b = sbanks.pop(t)
            pt, ptb = PTS[t % 4]
            ACT(lambda e: e.activation(out=pt, in_=s_ps, func=AF.Exp), r=[s_b], w=[ptb])
            for j in range(2):
                blk = 2 * kp + j
                MM(acc[0:65, 0:256], accb, Vp4[:, blk, a, 0:65], pt[:, j * 256:(j + 1) * 256], [Vpb, ptb], start=(blk == 0), stop=(blk == nkb - 1))
            if 2 * kp + 2 == nkb:
                ACT(lambda e: e.copy(out=o_sb[0:65, :], in_=acc[0:65, 0:256]), r=[accb], w=[o_sbb])
                V(lambda e: e.reciprocal(out=rden[64:65, :], in_=o_sb[64:65, :]), r=[o_sbb], w=[rdenb])
                return (g, a)
            return None

        pendF = []
        for t in range(min(LOOK, len(steps))):
            emit_S(t)
        for t in range(len(steps)):
            if t + LOOK < len(steps):
                emit_S(t + LOOK)
            for pf in pendF:
                pf[0] -= 1
            while pendF and pendF[0][0] <= 0:
                _, fg, fa = pendF.pop(0)
                emit_F2(fg, fa)
            fin = emit_rest(t)
            if fin is not None:
                pendF.append([2, fin[0], fin[1]])
        for _, fg, fa in pendF:
            emit_F2(fg, fa)

    if stop == "ph3":
        dump(mlaT.bitcast(F32), mlaTb, 4096)
        dump(QT.bitcast(F32)[:, 0:1024], QTb, 1024)
        dump(cosT[:, 0:256], csTb, 256)
        return finish()
    P.barrier()
    rot["n"] = 8
    C0 = m_carry

    def at_f32(off, cols):
        assert off + cols <= ACOLS, ("arena overflow", off + cols)
        return A.t[:, off:off + cols]

    def at_bf(off, cols):
        return at_f32(off, (cols + 1) // 2).bitcast(BF16)[:, 0:cols]

    A.top = m_4a
    w_o_sb = A.bf(8 * 1024)
    w_o3 = w_o_sb.rearrange("p (k n) -> p k n", k=8)
    gnw = A.f32(4)
    bc_post = A.f32(1024); bcb_ = Buf("bcast")
    HS = [(A.f32(1024), Buf("hsl%d" % i)) for i in range(2)]
    tmpf = A.f32(1024); tmpfb = Buf("tmpf")
    pfn = A.f32(8)
    E4a = A.top
    wg_sb = at_bf(C0, 8 * DFF)
    wpg_sb = at_bf(C0 + 11264, 8 * 1024)
    assert C0 + 11264 + 4096 <= m_ph12 - 4096
    wu_sb = at_bf(E4a, 8 * DFF)
    wp_sb = at_bf(E4a + 11264, 2 * 1024)
    SG = E4a + 12288
    O2 = SG + 2304
    wd_sb = at_bf(C0 + 15360, NFC * 1024)
    wg3 = wg_sb.rearrange("p (k n) -> p k n", k=8)
    wu3 = wu_sb.rearrange("p (k n) -> p k n", k=8)
    wd3 = wd_sb.rearrange("p (k n) -> p k n", k=NFC)
    wp3 = wp_sb.rearrange("p (k n) -> p k n", k=2)
    wpg3 = wpg_sb.rearrange("p (k n) -> p k n", k=8)
    w_ob = [Buf("w_o%d" % k) for k in range(8)]
    wgb = [Buf("wg%d" % k) for k in range(8)]
    wub = [Buf("wu%d" % k) for k in range(8)]
    wdb = [Buf("wd%d" % k) for k in range(NFC)]
    wpb = [Buf("wp%d" % k) for k in range(2)]
    wpgb = [Buf("wpg%d" % k) for k in range(8)]
    stg4a = [(at_f32(SG + i * 1408, 1408), Buf("stg4a%d" % i)) for i in range(5)]
    assert SG + 5 * 1408 <= ACOLS
    gnwb = Buf("gnw"); pfnb = Buf("pfn")
    P.dma(gnw, ret_gn_w.rearrange("(k p) -> p k", p=128), writes=[gnwb], allow_slow_non_contiguous=True)
    P.dma(pfn, pre_ffn_norm.rearrange("(k p) -> p k", p=128), writes=[pfnb], allow_slow_non_contiguous=True)
    P.dma(bc_post, post_mix_norm.partition_broadcast(128), writes=[bcb_])
    w_o_src = w_o.rearrange("(k p) n -> p k n", p=128)
    staged_load(w_o3[:, 0:4, :], w_ob[0:4], w_o_src[:, 0:4, :], 4, 1024, stg4a, 1024, scale_col=gnw, scale_buf=gnwb)
    staged_load(w_o3[:, 4:8, :], w_ob[4:8], w_o_src[:, 4:8, :], 4, 1024, stg4a, 1024)
    def sw_load_scaled(dst, dbuf, src, sc, scb):
        P.dma(dst, src, writes=[dbuf], eng="gpsimd")
        if sc is not None:
            G(lambda e: e.tensor_scalar(out=dst, in0=dst, scalar1=sc, scalar2=None, op0=ALU.mult), r=[dbuf, scb], w=[dbuf])

    wg_src = w_gate.rearrange("(k p) n -> p k n", p=128)
    wu_src = w_up.rearrange("(k p) n -> p k n", p=128)
    prefetch = []
    for k in range(8):
        prefetch.append(lambda k=k: staged_load(wg3[:, k:k + 1, :], wgb[k:k + 1], wg_src[:, k:k + 1, :], 1, DFF, stg4a, 1408, scale_col=pfn[:, k:k + 1], scale_buf=pfnb))
        prefetch.append(lambda k=k: staged_load(wu3[:, k:k + 1, :], wub[k:k + 1], wu_src[:, k:k + 1, :], 1, DFF, stg4a, 1408, scale_col=pfn[:, k:k + 1], scale_buf=pfnb))
    wpg_src = w_ple_gate.rearrange("(k p) n -> p k n", p=128)
    wp_src = w_ple_proj.rearrange("(k p) n -> p k n", p=128)
    for k in range(8):
        prefetch.append(lambda k=k: staged_load(wpg3[:, k:k + 1, :], wpgb[k:k + 1], wpg_src[:, k:k + 1, :], 1, 1024, stg4a, 1024))
    for k in range(2):
        prefetch.append(lambda k=k: staged_load(wp3[:, k:k + 1, :], wpb[k:k + 1], wp_src[:, k:k + 1, :], 1, 1024, stg4a, 1024))

    rt3 = retT.rearrange("p (k t) -> p k t", k=4)
    hsb = [Buf("hs%d" % i) for i in range(NOWN)]
    pss4 = {}

    def A4a(ib):
        xs, xb = HS[ib % 2]
        P.dma(xs, x_own[ib * 128:(ib + 1) * 128, :], writes=[xb])
        for _ in range(2):
            if prefetch:
                prefetch.pop(0)()
        pss = []
        for half in range(2):
            ps, pb = bank()
            for kc in range(4):
                MM(ps, pb, rt3[:, kc, ib * 128:(ib + 1) * 128], w_o3[:, kc, half * 512:(half + 1) * 512], [retTb, w_ob[kc]], start=(kc == 0), stop=False)
            for kc in range(4):
                MM(ps, pb, mlaT3[:, kc, ib * 128:(ib + 1) * 128], w_o3[:, 4 + kc, half * 512:(half + 1) * 512], [mlaTb, w_ob[4 + kc]], start=False, stop=(kc == 3))
            pss.append((ps, pb))
        pss4[ib] = pss

    def B4a_(ib):
        xs, xb = HS[ib % 2]
        pss = pss4.pop(ib)
        r, rb = rms_r([(pss[0][0], 512), (pss[1][0], 512)], 1024.0, [pss[0][1], pss[1][1]])
        for half in range(2):
            ps, pb = pss[half]
            V(lambda e, ps=ps, half=half: e.scalar_tensor_tensor(out=tmpf[:, half * 512:(half + 1) * 512], in0=ps, scalar=r, in1=bc_post[:, half * 512:(half + 1) * 512], op0=ALU.mult, op1=ALU.mult),
              r=[pb, rb, bcb_], w=[tmpfb])
        V(lambda e: e.tensor_tensor(out=xs, in0=xs, in1=tmpf, op=ALU.add), r=[xb, tmpfb], w=[xb])
        P.dma(hs[ib * 128:(ib + 1) * 128, :], xs, reads=[xb], writes=[hsb[ib]])

    A4a(0)
    for ib in range(NOWN):
        if ib + 1 < NOWN:
            A4a(ib + 1)
        B4a_(ib)
    while prefetch:
        prefetch.pop(0)()

    P.barrier()
    stg4c = [(at_f32(SG, 1024), Buf("stg4c")), (at_f32(SG + 1024, 1024), Buf("stg4d"))]
    gsb = at_f32(SG, 1024); gsbb = Buf("gsb")
    h2bf = at_bf(SG + 1024, 1024); h2bfb = Buf("h2bf")
    h2T = at_bf(SG + 1536, 1024); h2Tb = Buf("h2T")
    pbf = at_bf(SG + 2048, 256); pbfb = Buf("pbf")
    pT = at_bf(SG + 2176, 256); pTb = Buf("pT")
    M0 = C0 + 15360 + 11264
    bc_postffn = at_f32(M0, 1024); bc_ple = at_f32(M0 + 1024, 1024); bcb2 = Buf("bcast2")
    PB = [(at_f32(M0 + 4096, 256), Buf("pb0"))]
    sgf = at_f32(M0 + 4352, 128); sgfb = Buf("sgf")
    hnT = at_bf(M0 + 4480, 8 * 128); hnTb = Buf("hnT")
    hnT3 = hnT.rearrange("p (k t) -> p k t", k=8)
    assert M0 + 4992 <= E4a, (M0 + 4992, E4a)
    ACTT = []
    for i in range(2):
        a_ = at_bf(O2 + i * 1408, NFC * 128)
        ACTT.append((a_.rearrange("p (f t) -> p f t", f=NFC), Buf("actT%d" % i)))
    H2 = [(at_f32(M0 + 2048, 1024), Buf("h2_0")), (at_f32(M0 + 3072, 1024), Buf("h2_1")),
          (at_f32(O2 + 2816, 1024), Buf("h2_2")), (at_f32(O2 + 3840, 1024), Buf("h2_3"))]
    assert O2 + 4864 <= ACOLS, (O2 + 4864, ACOLS)
    b_row = gam.bitcast(BF16)[:, 0:1024]; browb = Buf("b_row")

    def hn_for(ib):
        if ib < 3:
            return H2[3][0][:, 0:512].bitcast(BF16), H2[3][1]
        return h2bf, h2bfb

    for ap_, src in ((bc_postffn, post_ffn_norm), (bc_ple, ple_norm)):
        P.dma(ap_, src.partition_broadcast(128), writes=[bcb2])
    P.dma(H2[2][0][0:1, :], b_ple_gate.rearrange("(o n) -> o n", o=1), writes=[H2[2][1]])
    V(lambda e: e.tensor_copy(out=b_row[0:1, :], in_=H2[2][0][0:1, :]), r=[H2[2][1]], w=[browb])
    wd_src = w_down.rearrange("(k p) n -> p k n", p=128)
    wd_jobs = [lambda k=k: staged_load(wd3[:, k:k + 1, :], wdb[k:k + 1], wd_src[:, k:k + 1, :], 1, 1024, stg4c, 1024) for k in range(NFC)]

    def stage_A4s(ib):
        hx, hxb = H2[ib % 4]
        hn_, hnb_ = hn_for(ib)
        P.dma(hx, hs[ib * 128:(ib + 1) * 128, :], reads=[hsb[ib]], writes=[hxb])
        r, rb = rms_r([(hx, 1024)], 1024.0, [hxb])
        V(lambda e: e.tensor_scalar(out=hn_, in0=hx, scalar1=r, scalar2=None, op0=ALU.mult), r=[hxb, rb], w=[hnb_])

    def stage_A4(ib):
        hn_, hnb_ = hn_for(ib)
        ps, pb = bank()
        psb = ps.bitcast(BF16)
        for kc in range(8):
            TR(psb[:, kc * 128:(kc + 1) * 128], pb, hn_[:, kc * 128:(kc + 1) * 128], [hnb_])
        ACT(lambda e: e.copy(out=hnT, in_=psb), r=[pb], w=[hnTb])
        actT3, actTb = ACTT[ib % 2]
        for fc in range(NFC):
            ps, pb = bank()
            for kc in range(8):
                MM(ps[:, 0:128], pb, wg3[:, kc, fc * 128:(fc + 1) * 128], hnT3[:, kc, :], [wgb[kc], hnTb], start=(kc == 0), stop=(kc == 7))
            for kc in range(8):
                MM(ps[:, 128:256], pb, wu3[:, kc, fc * 128:(fc + 1) * 128], hnT3[:, kc, :], [wub[kc], hnTb], start=(kc == 0), stop=(kc == 7))
            ACT(lambda e, ps=ps: e.activation(out=sgf, in_=ps[:, 0:128], func=AF.Silu), r=[pb], w=[sgfb])
            V(lambda e, ps=ps, fc=fc: e.tensor_tensor(out=actT3[:, fc, :], in0=sgf, in1=ps[:, 128:256], op=ALU.mult), r=[sgfb, pb], w=[actTb])

    def stage_B4a(ib):
        hx, hxb = H2[ib % 4]
        pp, ppb = PB[0]
        P.dma(pp, p_own[ib * 128:(ib + 1) * 128, :], writes=[ppb])
        actT3, actTb = ACTT[ib % 2]
        pss = []
        for half in range(2):
            ps, pb = bank()
            for fc in range(NFC):
                MM(ps, pb, actT3[:, fc, :], wd3[:, fc, half * 512:(half + 1) * 512], [actTb, wdb[fc]], start=(fc == 0), stop=(fc == NFC - 1))
            pss.append((ps, pb))
        r, rb = rms_r([(pss[0][0], 512), (pss[1][0], 512)], 1024.0, [pss[0][1], pss[1][1]])
        for half in range(2):
            ps, pb = pss[half]
            V(lambda e, ps=ps, half=half: e.scalar_tensor_tensor(out=gsb[:, half * 512:(half + 1) * 512], in0=ps, scalar=r, in1=bc_postffn[:, half * 512:(half + 1) * 512], op0=ALU.mult, op1=ALU.mult),
              r=[pb, rb, bcb2], w=[gsbb])
        G(lambda e: e.tensor_tensor(out=hx, in0=hx, in1=gsb, op=ALU.add), r=[hxb, gsbb], w=[hxb])

    def stage_B4b(ib):
        hx, hxb = H2[ib % 4]
        pp, ppb = PB[0]
        ACT(lambda e: e.copy(out=h2bf, in_=hx), r=[hxb], w=[h2bfb])
        ps, pb = bank()
        psb = ps.bitcast(BF16)
        for kc in range(8):
            TR(psb[:, kc * 128:(kc + 1) * 128], pb, h2bf[:, kc * 128:(kc + 1) * 128], [h2bfb])
        V(lambda e: e.tensor_copy(out=h2T, in_=psb), r=[pb], w=[h2Tb])
        h2T3 = h2T.rearrange("p (k t) -> p k t", k=8)
        for half in range(2):
            ps, pb = bank()
            for kc in range(8):
                MM(ps, pb, h2T3[:, kc, :], wpg3[:, kc, half * 512:(half + 1) * 512], [h2Tb, wpgb[kc]], start=(kc == 0), stop=False)
            MM(ps, pb, cmask[0:1, 0:128], b_row[0:1, half * 512:(half + 1) * 512], [constb, browb], start=False, stop=True)
            ACT(lambda e, ps=ps, half=half: e.activation(out=gsb[:, half * 512:(half + 1) * 512], in_=ps, func=AF.Sigmoid), r=[pb], w=[gsbb])
        V(lambda e: e.tensor_copy(out=pbf, in_=pp), r=[ppb], w=[pbfb])
        tp, tpb = bank()
        tpbf = tp.bitcast(BF16)
        for kc in range(2):
            TR(tpbf[:, kc * 128:(kc + 1) * 128], tpb, pbf[:, kc * 128:(kc + 1) * 128], [pbfb])
        V(lambda e: e.tensor_copy(out=pT, in_=tpbf[:, 0:256]), r=[tpb], w=[pTb])
        pT3 = pT.rearrange("p (k t) -> p k t", k=2)
        pse = []
        for half in range(2):
            ps, pb = bank()
            for kc in range(2):
                MM(ps, pb, pT3[:, kc, :], wp3[:, kc, half * 512:(half + 1) * 512], [pTb, wpb[kc]], start=(kc == 0), stop=(kc == 1))
            pse.append((ps, pb))
        r2, rb2 = rms_r([(pse[0][0], 512), (pse[1][0], 512)], 1024.0, [pse[0][1], pse[1][1]])
        for half in range(2):
            ps, pb = pse[half]
            V(lambda e, ps=ps, half=half: e.scalar_tensor_tensor(out=gsb[:, half * 512:(half + 1) * 512], in0=ps, scalar=r2, in1=gsb[:, half * 512:(half + 1) * 512], op0=ALU.mult, op1=ALU.mult),
              r=[pb, rb2, gsbb], w=[gsbb])
        V(lambda e: e.tensor_tensor(out=gsb, in0=gsb, in1=bc_ple, op=ALU.mult), r=[gsbb, bcb2], w=[gsbb])
        V(lambda e: e.tensor_tensor(out=gsb, in0=hx, in1=gsb, op=ALU.add), r=[hxb, gsbb], w=[gsbb])
        P.dma(out[ib * 128:(ib + 1) * 128, :], gsb, reads=[gsbb], is_output=True)

    stage_A4s(0)
    stage_A4(0)
    for j in wd_jobs:
        j()
    stage_A4s(1)
    stage_A4(1)
    stage_A4s(2)
    for ib in range(NOWN):
        stage_B4a(ib)
        if ib + 2 < NOWN:
            stage_A4(ib + 2)
        stage_B4b(ib)
        if ib + 3 < NOWN:
            stage_A4s(ib + 3)

    return finish()


_NC_CACHE = {}


def _rope_inv():
    inv_r = (1.0 / (np.float32(10000.0) ** (np.arange(64, dtype=np.float32) / np.float32(64)))).astype(np.float32)
    inv_m = (1.0 / (np.float32(10000.0) ** (np.arange(16, dtype=np.float32) / np.float32(16)))).astype(np.float32)
    return np.ascontiguousarray(np.broadcast_to(np.concatenate([inv_r, inv_m])[None, :], (128, 80))).astype(np.float32)


def _own_blocks(j):
    blks = []
    for g in range(NG):
        blks.append(8 * g + j)
        blks.append(8 * g + 7 - j)
    return blks


def kernel(**inputs):
    x = np.ascontiguousarray(np.asarray(inputs["x"], dtype=np.float32))
    p = np.ascontiguousarray(np.asarray(inputs["p"], dtype=np.float32))[0]
    positions = np.asarray(inputs["positions"]).astype(np.int32)
    wnames = ["pre_mix_norm", "w_in", "ret_gn_w", "mla_q_norm", "w_uq", "mla_kv_norm", "w_ukv", "w_o", "post_mix_norm",
              "pre_ffn_norm", "w_gate", "w_up", "w_down", "post_ffn_norm", "w_ple_proj", "ple_norm", "w_ple_gate", "b_ple_gate"]
    W = {n: np.ascontiguousarray(np.asarray(inputs[n], dtype=np.float32)[0]) for n in wnames}
    if "nc" not in _NC_CACHE:
        _NC_CACHE["nc"] = build_program()
    nc = _NC_CACHE["nc"]
    in_maps = []
    owns = []
    for c in range(8):
        b, j = c // 4, c % 4
        blks = _own_blocks(j)
        owns.append((b, blks))
        rows = np.concatenate([np.arange(k * 128, (k + 1) * 128) for k in blks])
        m = {
            "x_all": x[b],
            "x_own": np.ascontiguousarray(x[b][rows]),
            "p_own": np.ascontiguousarray(p[b][rows]),
            "pos_all": np.ascontiguousarray(positions[b].reshape(NBLK, 128).T),
            "pos_own": np.ascontiguousarray(positions[b][rows].reshape(NOWN, 128).T),
        }
        ws = np.zeros((128, 16), np.float32)
        ws[:, j] = 1.0
        ws[:, 8 + 7 - j] = 1.0
        m["wsel"] = ws
        am = np.zeros((128, 8, 256), np.float32)
        tri = (np.arange(128)[:, None] <= np.arange(128)[None, :]).astype(np.float32)
        for i in range(8):
            for s_, qb_ in enumerate((j, 7 - j)):
                if i < qb_:
                    am[:, i, s_ * 128:(s_ + 1) * 128] = 1.0
                elif i == qb_:
                    am[:, i, s_ * 128:(s_ + 1) * 128] = tri
        m["amask"] = am.reshape(128, 8 * 256)
        m["rope_inv"] = _rope_inv()
        m.update(W)
        in_maps.append(m)
    res = run_bass_kernel_spmd(nc, in_maps, core_ids=list(range(8)))
    out = np.empty((2, S, D), np.float32)
    for c in range(8):
        b, blks = owns[c]
        o = res.results[c]["out"]
        for k, blk in enumerate(blks):
            out[b, blk * 128:(blk + 1) * 128, :] = o[k * 128:(k + 1) * 128, :]
    return out
```

```python
import contextlib
import numpy as np
import concourse.bass as bass
import concourse.mybir as mybir
from concourse.bass_utils import run_bass_kernel_spmd

F32 = mybir.dt.float32
BF16 = mybir.dt.bfloat16
I32 = mybir.dt.int32
AF = mybir.ActivationFunctionType
ALU = mybir.AluOpType

ENGS = ("tensor", "vector", "scalar", "gpsimd", "sync")


class Buf:
    __slots__ = ("name", "last_w", "readers", "wsem", "wcnt", "rsem", "rcnt", "excl")

    def __init__(self, name, excl=False):
        self.name = name
        self.excl = excl
        self.last_w = None
        self.readers = []
        self.wsem = None
        self.wcnt = 0
        self.rsem = None
        self.rcnt = 0


class Op:
    __slots__ = ("eng", "seq", "fn", "waits", "sem", "val", "is_dma", "signal", "is_pe_acc")

    def __init__(self, eng, seq, fn, is_dma):
        self.eng = eng
        self.seq = seq
        self.fn = fn
        self.waits = []
        self.sem = None
        self.val = None
        self.is_dma = is_dma
        self.signal = False


class Prog:
    def __init__(self, nc):
        self.nc = nc
        self.stack = contextlib.ExitStack()
        self.ops = {e: [] for e in ENGS}
        self.esem = {}
        self.nsem = 0
        for e in ENGS:
            if e != "sync":
                self.esem[e] = self.new_sem("e_" + e)
        self.waited = {e: {p: -1 for p in ENGS} for e in ENGS}
        self.waited_sem = {e: {} for e in ENGS}
        self.out_dmas = []
        self.dma_latest = {}

    def new_sem(self, name):
        self.nsem += 1
        return self.stack.enter_context(self.nc.semaphore(name))

    def sbuf(self, name, shape, dtype):
        return self.stack.enter_context(self.nc.sbuf_tensor(name, shape, dtype))

    def psum(self, name, shape, dtype):
        return self.stack.enter_context(self.nc.psum_tensor(name, shape, dtype))

    def _dep(self, op, prod, same_eng_ok):
        if prod is None or prod is op:
            return
        ce = op.eng
        if prod.is_dma:
            key = id(prod.sem)
            cur = self.waited_sem[ce].get(key, 0)
            if prod.val > cur:
                self.waited_sem[ce][key] = prod.val
                op.waits.append(prod)
            return
        pe = prod.eng
        if pe == ce and not op.is_dma:
            if same_eng_ok:
                return
        if prod.seq > self.waited[ce][pe]:
            self.waited[ce][pe] = prod.seq
            prod.signal = True
            op.waits.append(prod)

    def add(self, eng, fn, reads=(), writes=(), pe_acc=False):
        op = Op(eng, len(self.ops[eng]), fn, False)
        for b in reads:
            self._dep(op, b.last_w, False)
            if b.excl:
                for r in b.readers:
                    if r.eng != eng:
                        self._dep(op, r, True)
        for b in writes:
            if b.last_w is not None:
                self._dep(op, b.last_w, pe_acc and b.last_w.eng == "tensor" and eng == "tensor")
            for r in b.readers:
                self._dep(op, r, eng != "gpsimd")
        for b in reads:
            b.readers.append(op)
        for b in writes:
            b.last_w = op
            b.readers = []
        self.ops[eng].append(op)
        return op

    def dma(self, out, in_, reads=(), writes=(), eng="sync", is_output=False, **kw):
        def fn(e):
            return e.dma_start(out=out, in_=in_, **kw)
        op = Op(eng, len(self.ops[eng]), fn, True)
        for b in reads:
            self._dep(op, b.last_w, False)
        for b in writes:
            if b.last_w is not None and not b.last_w.is_dma:
                self._dep(op, b.last_w, False)
            for r in b.readers:
                self._dep(op, r, False)
        if eng == "gpsimd":
            op.sem, op.val = self.new_sem("sw%d" % self.nsem), 16
            for bb in writes:
                bb.last_w = op
                bb.readers = []
        elif writes:
            b = writes[0]
            if b.wsem is None:
                b.wsem = self.new_sem("w_" + b.name)
            b.wcnt += 16
            op.sem, op.val = b.wsem, b.wcnt
            for bb in writes:
                bb.last_w = op
                bb.readers = []
        else:
            b = reads[0]
            if b.rsem is None:
                b.rsem = self.new_sem("r_" + b.name)
            b.rcnt += 16
            op.sem, op.val = b.rsem, b.rcnt
        for b in reads:
            b.readers.append(op)
        if is_output:
            self.out_dmas.append(op)
        self.dma_latest[id(op.sem)] = op
        self.ops[eng].append(op)
        return op

    def barrier(self):
        lasts = {}
        for x in ENGS:
            for o in reversed(self.ops[x]):
                if not o.is_dma and o.fn is not None:
                    lasts[x] = o
                    break
        dmas = list(self.dma_latest.values())
        for e in ENGS:
            op = Op(e, len(self.ops[e]), None, False)
            for x, l in lasts.items():
                self._dep(op, l, False)
            for d in dmas:
                self._dep(op, d, False)
            self.ops[e].append(op)

    def emit(self):
        nc = self.nc
        fin = Op("sync", len(self.ops["sync"]), None, False)
        seen = {}
        for o in self.out_dmas:
            seen[id(o.sem)] = max(seen.get(id(o.sem), (0, None))[0], o.val), o.sem
        for e in ENGS:
            cnt = 0
            for op in self.ops[e]:
                if op.is_dma or op.fn is None:
                    continue
                if op.signal:
                    cnt += 1
                    op.sem = self.esem[e]
                    op.val = cnt
        ops = self.ops
        finals = list(seen.values())

        with nc.Block() as block:
            def run(e, name):
                for op in ops[name]:
                    for w in op.waits:
                        e.wait_ge(w.sem, w.val)
                    if op.fn is None:
                        continue
                    ins = op.fn(e)
                    if op.is_dma:
                        ins.then_inc(op.sem, 16)
                    elif op.signal:
                        ins.then_inc(op.sem, 1)
                if name == "sync":
                    for val, sem in finals:
                        e.wait_ge(sem, val)

            @block.sync
            def _(e):
                run(e, "sync")

            @block.tensor
            def _(e):
                run(e, "tensor")

            @block.vector
            def _(e):
                run(e, "vector")

            @block.scalar
            def _(e):
                run(e, "scalar")

            @block.gpsimd
            def _(e):
                run(e, "gpsimd")

    def close(self):
        self.stack.close()


import math

D = 1024
S = 8192
NBLK = 64
NG = 8
NOWN = 16
INC = 2720
DFF = 2816
NFC = 22
EPS = 1e-6
PI = math.pi
RET_LOGG = [math.log(1.0 - 2.0 ** (-5.0 - h)) for h in range(4)]
SM_SCALE = 1.0 / math.sqrt(96.0)


class _Stop(Exception):
    pass


_LAST = {}


class Arena:
    def __init__(self, tile, ncols):
        self.t = tile
        self.n = ncols
        self.top = 0

    def f32(self, cols):
        off = self.top
        self.top += cols
        assert self.top <= self.n, ("arena overflow", self.top, self.n)
        return self.t[:, off:off + cols]

    def bf(self, cols):
        c = (cols + 1) // 2
        return self.f32(c).bitcast(BF16)[:, 0:cols]

    def i32(self, cols):
        return self.f32(cols).bitcast(I32)


def build_program(debug=False, ng_run=NG, stop=None):
    nc = bass.Bass("TRN2", target_bir_lowering=False)

    def din(name, shape, dt=F32):
        return nc.dram_tensor(name, shape, dt, kind="ExternalInput").ap()

    x_all = din("x_all", [S, D])
    x_own = din("x_own", [NOWN * 128, D])
    p_own = din("p_own", [NOWN * 128, 256])
    pos_all = din("pos_all", [128, NBLK], I32)
    pos_own = din("pos_own", [128, NOWN], I32)
    wsel = din("wsel", [128, 16])
    amask = din("amask", [128, 8 * 256])
    rope_inv = din("rope_inv", [128, 80])
    pre_mix_norm = din("pre_mix_norm", [D])
    w_in = din("w_in", [D, INC])
    ret_gn_w = din("ret_gn_w", [512])
    mla_q_norm = din("mla_q_norm", [384])
    w_uq = din("w_uq", [384, 768])
    mla_kv_norm = din("mla_kv_norm", [256])
    w_ukv = din("w_ukv", [256, 1024])
    w_o = din("w_o", [D, D])
    post_mix_norm = din("post_mix_norm", [D])
    pre_ffn_norm = din("pre_ffn_norm", [D])
    w_gate = din("w_gate", [D, DFF])
    w_up = din("w_up", [D, DFF])
    w_down = din("w_down", [DFF, D])
    post_ffn_norm = din("post_ffn_norm", [D])
    w_ple_proj = din("w_ple_proj", [256, D])
    ple_norm = din("ple_norm", [D])
    w_ple_gate = din("w_ple_gate", [D, D])
    b_ple_gate = din("b_ple_gate", [D])
    out = nc.dram_tensor("out", [NOWN * 128, D], F32, kind="ExternalOutput").ap()
    dbg = nc.dram_tensor("dbg", [128, 16384], F32, kind="ExternalOutput").ap() if debug else None
    dbg_off = {"n": 0}

    def dump(ap, buf, cols):
        o = dbg_off["n"]
        dbg_off["n"] += cols
        P.dma(dbg[:, o:o + cols], ap, reads=[buf], is_output=True)
        return o

    def finish():
        P.emit()
        P.close()
        return nc

    _LAST["nc"] = nc

    chk_cnt = {}

    def chk(name):
        chk_cnt[name] = chk_cnt.get(name, 0) + 1
        if stop == name or stop == "%s:%d" % (name, chk_cnt[name]):
            dump(stat, statb[0], 64)
            P.emit()
            P.close()
            raise _Stop()
    hs = nc.dram_tensor("hs", [NOWN * 128, D], F32, kind="Internal").ap()

    P = Prog(nc)
    ACOLS = 53184
    A = Arena(P.sbuf("arena", [128, ACOLS], F32)[:, :], ACOLS)

    banks = [(P.psum("ps%d" % i, [128, 512], F32)[:, :], Buf("ps%d" % i, excl=True)) for i in range(8)]
    rot = {"i": 0, "n": 6}

    def bank():
        b = banks[rot["i"] % rot["n"]]
        rot["i"] += 1
        return b

    defer = {"on": False, "list": []}

    def _rec(eng, fn, r, w):
        if defer["on"]:
            defer["list"].append((eng, fn, list(r), list(w)))
            return None
        return P.add(eng, fn, reads=r, writes=w)

    def drip(k):
        for _ in range(min(k, len(defer["list"]))):
            eng, fn, r, w = defer["list"].pop(0)
            P.add(eng, fn, reads=r, writes=w)

    def V(fn, r=(), w=()):
        return _rec("vector", fn, r, w)

    def G(fn, r=(), w=()):
        return _rec("gpsimd", fn, r, w)

    def ACT(fn, r=(), w=()):
        return _rec("scalar", fn, r, w)

    def MM(out_ap, ob, lhsT, rhs, r, start=True, stop=True):
        return P.add("tensor", lambda e: e.matmul(out_ap, lhsT=lhsT, rhs=rhs, start=start, stop=stop),
                     reads=r, writes=[ob], pe_acc=True)

    def TR(out_ap, ob, in_ap, r):
        idn = ident if in_ap.dtype == BF16 else identf
        return P.add("tensor", lambda e: e.transpose(out=out_ap, in_=in_ap, identity=idn[0:in_ap.shape[0], 0:in_ap.shape[0]]),
                     reads=list(r) + [constb], writes=[ob], pe_acc=True)

    if debug:
        P.add("gpsimd", lambda e: e.memset(A.t, 0.0), writes=[])
        P.barrier()
    constb = Buf("const")
    ident = A.bf(128)
    identf = A.f32(128)
    cmask = A.bf(128)
    ones_t = A.f32(128)
    epsc = A.f32(1)
    pidx = A.f32(1)
    dq = A.f32(4)
    dk = A.f32(4)
    gam = A.f32(512)
    inv_r = A.f32(64)
    inv_m = A.f32(16)
    tmpi = A.i32(64)
    stat = A.f32(64)
    statb = [Buf("stat%d" % i) for i in range(8)]
    stat_i = {"i": 0}

    def stats():
        i = stat_i["i"] % 8
        stat_i["i"] += 1
        return stat[:, i * 8:(i + 1) * 8], statb[i]

    G(lambda e: e.memset(ones_t, 1.0), w=[constb])
    G(lambda e: e.memset(epsc, EPS), w=[constb])
    G(lambda e: e.affine_select(out=identf, in_=ones_t, pattern=[[-1, 128]], compare_op=ALU.is_equal, fill=0.0, base=0, channel_multiplier=1), r=[constb], w=[constb])
    G(lambda e: e.tensor_copy(out=ident, in_=identf), r=[constb], w=[constb])
    G(lambda e: e.affine_select(out=cmask, in_=ones_t, pattern=[[1, 128]], compare_op=ALU.is_ge, fill=0.0, base=0, channel_multiplier=-1), r=[constb], w=[constb])
    G(lambda e: e.iota(tmpi[:, 0:1], pattern=[[0, 1]], base=0, channel_multiplier=1), w=[constb])
    G(lambda e: e.tensor_copy(out=pidx, in_=tmpi[:, 0:1]), r=[constb], w=[constb])
    for h in range(4):
        ACT(lambda e, h=h: e.activation(out=dq[:, h:h + 1], in_=pidx, func=AF.Exp, scale=RET_LOGG[h]), r=[constb], w=[constb])
        ACT(lambda e, h=h: e.activation(out=dk[:, h:h + 1], in_=pidx, func=AF.Exp, scale=-RET_LOGG[h]), r=[constb], w=[constb])
        G(lambda e, h=h: e.memset(gam[:, h * 128:(h + 1) * 128], math.exp(128 * RET_LOGG[h])), w=[constb])
    V(lambda e: e.tensor_scalar(out=dk, in0=dk, scalar1=128.0 ** -0.5, scalar2=None, op0=ALU.mult), r=[constb], w=[constb])
    invb = Buf("inv")
    P.dma(inv_r, rope_inv[:, 0:64], writes=[invb])
    P.dma(inv_m, rope_inv[:, 64:80], writes=[invb])

    junk = A.bf(1024)

    def rms_r(srcs, n, rd):
        st, sb = stats()
        for j, (ap, wdt) in enumerate(srcs):
            ACT(lambda e, ap=ap, wdt=wdt, j=j: e.activation(out=junk[:, 0:wdt], in_=ap, func=AF.Square, accum_out=st[:, j:j + 1]),
                r=rd, w=[sb])
        if len(srcs) == 2:
            V(lambda e: e.tensor_tensor(out=st[:, 0:1], in0=st[:, 0:1], in1=st[:, 1:2], op=ALU.add), r=[sb], w=[sb])
        ACT(lambda e: e.activation(out=st[:, 2:3], in_=st[:, 0:1], func=AF.Sqrt, bias=epsc, scale=1.0 / n), r=[sb, constb], w=[sb])
        V(lambda e: e.reciprocal(out=st[:, 3:4], in_=st[:, 2:3]), r=[sb], w=[sb])
        return st[:, 3:4], sb

    def build_tables(posf, nb, half, inv, Cf, Sf, T4, rd, wb, tb):
        n = nb * half
        a3 = lambda t: t[:, 0:n].rearrange("p (b j) -> p b j", b=nb)
        Aa, Bb, Cc, Dd = [t[:, 0:n] for t in T4]
        Bi = Bb.bitcast(I32)
        V(lambda e: e.tensor_tensor(out=a3(T4[0]), in0=posf.unsqueeze(2).to_broadcast([128, nb, half]),
                                    in1=inv.unsqueeze(1).to_broadcast([128, nb, half]), op=ALU.mult), r=list(rd) + [invb], w=[tb])
        V(lambda e: e.tensor_scalar(out=Bi, in0=Aa, scalar1=1.0 / (2 * PI), scalar2=None, op0=ALU.mult), r=[tb], w=[tb])
        V(lambda e: e.tensor_copy(out=Cc, in_=Bi), r=[tb], w=[tb])
        V(lambda e: e.scalar_tensor_tensor(out=Aa, in0=Cc, scalar=-2 * PI, in1=Aa, op0=ALU.mult, op1=ALU.add), r=[tb], w=[tb])
        V(lambda e: e.tensor_scalar(out=Bb, in0=Aa, scalar1=PI, scalar2=-2 * PI, op0=ALU.is_gt, op1=ALU.mult), r=[tb], w=[tb])
        V(lambda e: e.tensor_tensor(out=Cc, in0=Aa, in1=Bb, op=ALU.add), r=[tb], w=[tb])
        V(lambda e: e.tensor_scalar(out=Cc, in0=Cc, scalar1=-PI, scalar2=PI, op0=ALU.max, op1=ALU.min), r=[tb], w=[tb])
        V(lambda e: e.tensor_scalar(out=Dd, in0=Aa, scalar1=PI / 2, scalar2=None, op0=ALU.add), r=[tb], w=[tb])
        V(lambda e: e.tensor_scalar(out=Bb, in0=Dd, scalar1=PI, scalar2=-2 * PI, op0=ALU.is_gt, op1=ALU.mult), r=[tb], w=[tb])
        V(lambda e: e.tensor_tensor(out=Dd, in0=Dd, in1=Bb, op=ALU.add), r=[tb], w=[tb])
        V(lambda e: e.tensor_scalar(out=Dd, in0=Dd, scalar1=-PI, scalar2=PI, op0=ALU.max, op1=ALU.min), r=[tb], w=[tb])
        C3 = Cf.rearrange("p (b j) -> p b j", b=nb)
        S3 = Sf.rearrange("p (b j) -> p b j", b=nb)
        ACT(lambda e: e.activation(out=C3[:, :, 0:half], in_=a3(T4[3]), func=AF.Sin), r=[tb], w=[wb])
        ACT(lambda e: e.activation(out=C3[:, :, half:2 * half], in_=a3(T4[3]), func=AF.Sin), r=[tb], w=[wb])
        ACT(lambda e: e.activation(out=S3[:, :, 0:half], in_=a3(T4[2]), func=AF.Sin, scale=-1.0), r=[tb], w=[wb])
        ACT(lambda e: e.activation(out=S3[:, :, half:2 * half], in_=a3(T4[2]), func=AF.Sin), r=[tb], w=[wb])

    def rope_tm(eng, src3, Cb, Sb, nh, hd, t1, t2, out3, rd, tb, wb):
        half = hd // 2
        t13 = t1.rearrange("p (h d) -> p h d", h=nh)
        t23 = t2.rearrange("p (h d) -> p h d", h=nh)
        add = lambda fn, r, w: P.add(eng, fn, reads=r, writes=w)
        tbs = tb if isinstance(tb, list) else [tb]
        add(lambda e: e.tensor_tensor(out=t13, in0=src3, in1=Cb.unsqueeze(1).to_broadcast([128, nh, hd]), op=ALU.mult), list(rd), tbs)
        add(lambda e: e.tensor_tensor(out=t23[:, :, 0:half], in0=src3[:, :, half:hd], in1=Sb[:, 0:half].unsqueeze(1).to_broadcast([128, nh, half]), op=ALU.mult), list(rd), tbs)
        add(lambda e: e.tensor_tensor(out=t23[:, :, half:hd], in0=src3[:, :, 0:half], in1=Sb[:, half:hd].unsqueeze(1).to_broadcast([128, nh, half]), op=ALU.mult), list(rd), tbs)
        add(lambda e: e.tensor_tensor(out=out3, in0=t13, in1=t23, op=ALU.add), tbs, [wb])

    cast_i = {"n": 0}

    def staged_load(dst3, dbufs, src3, nk, ncols, stages, chunk_cols, scale_col=None, scale_buf=None, queue="sync"):
        for k in range(nk):
            c = 0
            while c < ncols:
                w = min(chunk_cols, ncols - c)
                sa, sb_ = stages[cast_i["n"] % len(stages)]
                q_ = queue if queue != "alt" else ("sync" if cast_i["n"] % 2 == 0 else "scalar")
                P.dma(sa[:, 0:w], src3[:, k, c:c + w], writes=[sb_], eng=q_)
                use_act = (cast_i["n"] % 2 == 0)
                cast_i["n"] += 1
                dst = dst3[:, k, c:c + w]
                src = sa[:, 0:w]
                rd = [sb_] + ([scale_buf] if scale_buf is not None else [])
                if scale_col is not None:
                    sc = scale_col[:, k:k + 1]
                    if use_act:
                        ACT(lambda e, dst=dst, src=src, sc=sc: e.activation(out=dst, in_=src, func=AF.Copy, scale=sc), r=rd, w=[dbufs[k]])
                    else:
                        V(lambda e, dst=dst, src=src, sc=sc: e.tensor_scalar(out=dst, in0=src, scalar1=sc, scalar2=None, op0=ALU.mult), r=rd, w=[dbufs[k]])
                else:
                    if use_act:
                        ACT(lambda e, dst=dst, src=src: e.copy(out=dst, in_=src), r=rd, w=[dbufs[k]])
                    else:
                        V(lambda e, dst=dst, src=src: e.tensor_copy(out=dst, in_=src), r=rd, w=[dbufs[k]])
                c += w

    m_carry = A.top
    ckvnT = A.bf(2 * S)
    ckvnTb = Buf("ckvnT")
    KT_A = A.bf(S)
    KT_Ab = Buf("KT_A")
    cqnT = A.bf(3 * 2048)
    cqnTb = Buf("cqnT")
    retT = A.bf(4 * 2048)
    retTb = Buf("retT")
    m_ph12 = A.top

    w_in_sb = A.bf(8 * INC)
    w_in3 = w_in_sb.rearrange("p (k n) -> p k n", k=8)
    pmn = A.f32(8)
    XS = [(A.f32(1024), Buf("xs%d" % i)) for i in range(3)]
    XNB = [(A.bf(1024), Buf("xnb%d" % i)) for i in range(2)]
    XNT = [(A.bf(1024), Buf("xnT%d" % i)) for i in range(3)]
    kd_ = A.f32(512); kdb_ = Buf("kd")
    t1_ = A.f32(512); t2_ = A.f32(512); ropeb_ = Buf("ropet")
    KB = [(A.bf(512), Buf("kb%d" % i)) for i in range(2)]
    VB = [(A.bf(512), Buf("vb%d" % i)) for i in range(2)]
    qb = A.bf(512); qbb = Buf("qb")
    qT = A.bf(512); qTb = Buf("qT")
    kT = A.bf(512); kTb = Buf("kT")
    PT = A.bf(512); PTb = Buf("PT")
    yn = A.f32(512); ynb = Buf("yn")
    sg = A.f32(512); sgb = Buf("sg")
    ro = A.bf(512); rob = Buf("ro")
    NTB = 10
    TABS = []
    for _i in range(2):
        TABS.append({"Cfr": A.f32(NTB * 128), "Sfr": A.f32(NTB * 128), "tabb": Buf("tab_r%d" % _i),
                     "Cfm": A.f32(8 * 32), "Sfm": A.f32(8 * 32), "tabmb": Buf("tab_m%d" % _i)})
    T4all = A.f32(4 * 5 * 64)
    T4 = [T4all[:, i * 320:(i + 1) * 320] for i in range(4)]; t4b = Buf("t4")
    posg = A.f32(NTB); posgb = Buf("posg")
    pos_all_i = A.i32(NBLK); pos_own_i = A.i32(NOWN)
    pos_all_f = A.f32(NBLK); pos_own_f = A.f32(NOWN); posb = Buf("pos")
    wsel_sb = A.f32(16); wselb = Buf("wsel")
    Rst = A.f32(512); Rb_ = Buf("R")
    Tt = A.f32(512); Ttb = Buf("T")
    Rlo = A.f32(512); Rhi = A.f32(512); Rselb = Buf("Rlohi")
    Rsel_bf = A.bf(1024); Rselbfb = Buf("Rsel_bf")
    ckvn = A.bf(256); ckvnb = Buf("ckvn")
    krst = A.bf(96); krstb = Buf("krst")
    krt = A.f32(64); krtb = Buf("krt")
    cqn = A.bf(384); cqnb = Buf("cqn")
    gnst = A.f32(32); gnstb = Buf("gnst")

    pmnb = Buf("pmn")
    P.dma(pmn, pre_mix_norm.rearrange("(k p) -> p k", p=128), writes=[pmnb], allow_slow_non_contiguous=True)
    ckv_f32 = ckvnT.bitcast(F32)
    stg0 = [(ckv_f32[:, i * INC:(i + 1) * INC], Buf("stg0_%d" % i)) for i in range(3)]
    w_inb = [Buf("w_in%d" % k) for k in range(8)]
    staged_load(w_in3, w_inb, w_in.rearrange("(k p) n -> p k n", p=128), 8, INC, stg0, INC, scale_col=pmn, scale_buf=pmnb)
    P.dma(pos_all_i, pos_all, writes=[posb])
    P.dma(pos_own_i, pos_own, writes=[posb])
    P.dma(wsel_sb, wsel, writes=[wselb])
    V(lambda e: e.tensor_copy(out=pos_all_f, in_=pos_all_i), r=[posb], w=[posb])
    V(lambda e: e.tensor_copy(out=pos_own_f, in_=pos_own_i), r=[posb], w=[posb])
    G(lambda e: e.memset(krst, 0.0), w=[krstb])
    G(lambda e: e.memset(Rst, 0.0), w=[Rb_])

    if stop == "ph0":
        dump(w_in_sb.bitcast(F32)[:, 0:512], w_inb[0], 512)
        dump(dk, constb, 4)
        dump(inv_r, invb, 64)
        dump(pos_all_f, posb, 64)
        return finish()
    xload = {"n": 0}

    def load_x(src_rows):
        xs, xb = XS[xload["n"] % 3]
        xload["n"] += 1
        P.dma(xs, src_rows, writes=[xb])
        return xs, xb

    def norm_scale(xs, xb, n):
        r, rb = rms_r([(xs, 1024)], 1024.0, [xb])
        xn, xnb_ = XNB[n % 2]
        V(lambda e: e.tensor_scalar(out=xn, in0=xs, scalar1=r, scalar2=None, op0=ALU.mult), r=[xb, rb], w=[xnb_])

    def transpose_x(n, bk):
        xn, xnb_ = XNB[n % 2]
        ps, pb = bk
        psb = ps.bitcast(BF16)
        for kc in range(8):
            TR(psb[:, kc * 128:(kc + 1) * 128], pb, xn[:, kc * 128:(kc + 1) * 128], [xnb_])
        xt, xtb = XNT[n % 3]
        ACT(lambda e: e.copy(out=xt, in_=psb), r=[pb], w=[xtb])
        return xt.rearrange("p (k t) -> p k t", k=8), xtb

    def proj(xt3, xtb, c0, ncols, bk):
        ps, pb = bk
        for kc in range(8):
            MM(ps[:, 0:ncols], pb, xt3[:, kc, :], w_in3[:, kc, c0:c0 + ncols], [xtb, w_inb[kc]], start=(kc == 0), stop=(kc == 7))
        return ps, pb

    def ret_rope(ps, pb, dscale, tidx, out_bf, outb, eng, g, alt=False):
        Cfr, Sfr, tabb = TABS[g % 2]["Cfr"], TABS[g % 2]["Sfr"], TABS[g % 2]["tabb"]
        if alt:
            kd, kdb, t1, t2, ropeb = T4all[:, 0:512], t4b, T4all[:, 512:1024], yn, t4b
            ropew = [t4b, ynb]
        else:
            kd, kdb, t1, t2, ropeb = kd_, kdb_, t1_, t2_, ropeb_
            ropew = [ropeb_]
        for h in range(4):
            ACT(lambda e, h=h: e.activation(out=kd[:, h * 128:(h + 1) * 128], in_=ps[:, h * 128:(h + 1) * 128], func=AF.Copy, scale=dscale[:, h:h + 1]),
                r=[pb, constb], w=[kdb])
        kd3 = kd.rearrange("p (h d) -> p h d", h=4)
        rope_tm(eng, kd3, Cfr[:, tidx * 128:(tidx + 1) * 128], Sfr[:, tidx * 128:(tidx + 1) * 128], 4, 128, t1, t2,
                out_bf.rearrange("p (h d) -> p h d", h=4), [kdb, tabb], ropew, outb)

    seq = []
    for g in range(ng_run):
        for i in range(8):
            seq.append(("all", g * 8 + i, g, i))
        seq.append(("own", 2 * g, g, 0))
        seq.append(("own", 2 * g + 1, g, 1))
    NSEQ = len(seq)

    def src_rows(k):
        kind, idx = seq[k][0], seq[k][1]
        return (x_all if kind == "all" else x_own)[idx * 128:(idx + 1) * 128, :]

    loaded = {}
    for k in range(min(2, NSEQ)):
        loaded[k] = load_x(src_rows(k))
    st = {}

    def S1(n):
        xs, xb = loaded.pop(n)
        if n + 2 < NSEQ:
            loaded[n + 2] = load_x(src_rows(n + 2))
        norm_scale(xs, xb, n)

    def S2(n):
        B4 = banks[4 * (n % 2):4 * (n % 2) + 4]
        xt3, xtb = transpose_x(n, B4[0])
        st[n] = {"xt3": xt3, "xtb": xtb, "B4": B4}

    def S3(n):
        kind, idx, g, i = seq[n]
        d = st[n]
        xt3, xtb, B4 = d["xt3"], d["xtb"], d["B4"]
        if kind == "all":
            d["rk"] = proj(xt3, xtb, 512, 512, B4[1])
            d["rv"] = proj(xt3, xtb, 1024, 512, B4[2])
            d["ck"] = proj(xt3, xtb, 2432, 288, B4[3])
        else:
            d["rq"] = proj(xt3, xtb, 0, 512, B4[1])
            d["rk"] = proj(xt3, xtb, 512, 512, B4[2])
            d["rv"] = proj(xt3, xtb, 1024, 512, B4[3])

    def tables(g):
        V(lambda e: e.tensor_copy(out=posg[:, 0:8], in_=pos_all_f[:, g * 8:(g + 1) * 8]), r=[posb], w=[posgb])
        V(lambda e: e.tensor_copy(out=posg[:, 8:10], in_=pos_own_f[:, 2 * g:2 * g + 2]), r=[posb], w=[posgb])
        tb_ = TABS[g % 2]
        build_tables(posg[:, 0:5], 5, 64, inv_r, tb_["Cfr"][:, 0:640], tb_["Sfr"][:, 0:640], T4, [posgb], tb_["tabb"], t4b)
        build_tables(posg[:, 5:10], 5, 64, inv_r, tb_["Cfr"][:, 640:1280], tb_["Sfr"][:, 640:1280], T4, [posgb], tb_["tabb"], t4b)
        build_tables(posg[:, 0:8], 8, 16, inv_m, tb_["Cfm"], tb_["Sfm"], T4, [posgb], tb_["tabmb"], t4b)

    def B_elem_all(n):
        kind, blk, g, i = seq[n]
        d = st[n]
        B4 = d["B4"]
        rk_ps, rk_b = d["rk"]; rv_ps, rv_b = d["rv"]; ck_ps, ck_b = d["ck"]
        kb_, kbb = KB[n % 2]
        vb_, vbb = VB[n % 2]
        r, rb = rms_r([(ck_ps[:, 0:256], 256)], 256.0, [ck_b])
        ret_rope(rk_ps, rk_b, dk, i, kb_, kbb, "gpsimd", g)
        ACT(lambda e: e.copy(out=vb_, in_=rv_ps), r=[rv_b], w=[vbb])
        V(lambda e: e.tensor_scalar(out=ckvn, in0=ck_ps[:, 0:256], scalar1=r, scalar2=None, op0=ALU.mult), r=[ck_b, rb], w=[ckvnb])
        V(lambda e: e.tensor_copy(out=krt[:, 0:32], in_=ck_ps[:, 256:288]), r=[ck_b], w=[krtb])
        Cfm, Sfm, tabmb = TABS[g % 2]["Cfm"], TABS[g % 2]["Sfm"], TABS[g % 2]["tabmb"]
        rope_tm("vector", krt[:, 0:32].rearrange("p (h d) -> p h d", h=1), Cfm[:, i * 32:(i + 1) * 32], Sfm[:, i * 32:(i + 1) * 32], 1, 32,
                krt[:, 32:64], gnst[:, 0:32], krst[:, 64:96].rearrange("p (h d) -> p h d", h=1), [krtb, tabmb], krtb, krstb)

    def B_state(n):
        kind, blk, g, i = seq[n]
        if kind != "all":
            return
        u_ps, u_b = ubank.pop(n)
        G(lambda e: e.tensor_tensor(out=Tt, in0=Rst, in1=gam, op=ALU.mult), r=[Rb_, constb], w=[Ttb])
        if i == 0:
            V(lambda e: e.tensor_scalar(out=Rlo, in0=Tt, scalar1=wsel_sb[:, 0:1], scalar2=None, op0=ALU.mult), r=[Ttb, wselb], w=[Rselb])
            V(lambda e: e.tensor_scalar(out=Rhi, in0=Tt, scalar1=wsel_sb[:, 8:9], scalar2=None, op0=ALU.mult), r=[Ttb, wselb], w=[Rselb])
        else:
            V(lambda e: e.scalar_tensor_tensor(out=Rlo, in0=Tt, scalar=wsel_sb[:, i:i + 1], in1=Rlo, op0=ALU.mult, op1=ALU.add), r=[Ttb, wselb, Rselb], w=[Rselb])
            V(lambda e: e.scalar_tensor_tensor(out=Rhi, in0=Tt, scalar=wsel_sb[:, 8 + i:9 + i], in1=Rhi, op0=ALU.mult, op1=ALU.add), r=[Ttb, wselb, Rselb], w=[Rselb])
        V(lambda e: e.tensor_tensor(out=Rst, in0=Tt, in1=u_ps, op=ALU.add), r=[Ttb, u_b], w=[Rb_])
        if i == 7:
            V(lambda e: e.tensor_copy(out=Rsel_bf[:, 0:512], in_=Rlo), r=[Rselb], w=[Rselbfb])
            V(lambda e: e.tensor_copy(out=Rsel_bf[:, 512:1024], in_=Rhi), r=[Rselb], w=[Rselbfb])

    ubank = {}

    def B_pe_all(n):
        kind, blk, g, i = seq[n]
        d = st.pop(n)
        B4 = d["B4"]
        kb_, kbb = KB[n % 2]
        vb_, vbb = VB[n % 2]
        u_ps, u_b = B4[2]
        for h in range(4):
            MM(u_ps[:, h * 128:(h + 1) * 128], u_b, kb_[:, h * 128:(h + 1) * 128], vb_[:, h * 128:(h + 1) * 128], [kbb, vbb])
        ubank[n] = (u_ps, u_b)
        tp, tpb = B4[0]
        tpbf = tp.bitcast(BF16)
        TR(tpbf[:, 0:128], tpb, ckvn[:, 0:128], [ckvnb])
        TR(tpbf[:, 128:256], tpb, ckvn[:, 128:256], [ckvnb])
        ck3_ = ckvnT.rearrange("p (k t) -> p k t", k=2)
        TR(tpbf[0:96, 256:384], tpb, krst, [krstb])
        V(lambda e: e.tensor_copy(out=ck3_[:, :, blk * 128:(blk + 1) * 128], in_=tpbf[:, 0:256].rearrange("p (k t) -> p k t", k=2)),
          r=[tpb], w=[ckvnTb])
        V(lambda e: e.tensor_copy(out=KT_A[64:96, blk * 128:(blk + 1) * 128], in_=tpbf[64:96, 256:384]), r=[tpb], w=[KT_Ab])

    def B_elem_own(n):
        kind, ib, g, s_ = seq[n]
        d = st[n]
        rq_ps, rq_b = d["rq"]; rk_ps, rk_b = d["rk"]; rv_ps, rv_b = d["rv"]
        kb_, kbb = KB[n % 2]
        vb_, vbb = VB[n % 2]
        ret_rope(rq_ps, rq_b, dq, 8 + s_, qb, qbb, "vector", g, alt=True)
        ret_rope(rk_ps, rk_b, dk, 8 + s_, kb_, kbb, "gpsimd", g)
        ACT(lambda e: e.copy(out=vb_, in_=rv_ps), r=[rv_b], w=[vbb])

    def B_pe_own(n):
        kind, ib, g, s_ = seq[n]
        d = st.pop(n)
        B4 = d["B4"]
        xt3, xtb = d["xt3"], d["xtb"]
        kb_, kbb = KB[n % 2]
        vb_, vbb = VB[n % 2]
        rg_ps, rg_b = proj(xt3, xtb, 1536, 512, B4[0])
        cq_ps, cq_b = proj(xt3, xtb, 2048, 384, B4[1])
        ACT(lambda e: e.activation(out=sg, in_=rg_ps, func=AF.Silu), r=[rg_b], w=[sgb])
        r, rb = rms_r([(cq_ps[:, 0:384], 384)], 384.0, [cq_b])
        V(lambda e: e.tensor_scalar(out=cqn, in0=cq_ps[:, 0:384], scalar1=r, scalar2=None, op0=ALU.mult), r=[cq_b, rb], w=[cqnb])
        tq, tqb = B4[2]
        tqbf = tq.bitcast(BF16)
        for h in range(4):
            TR(tqbf[:, h * 128:(h + 1) * 128], tqb, qb[:, h * 128:(h + 1) * 128], [qbb])
        for h in range(4):
            TR(tqbf[:, 512 + h * 128:512 + (h + 1) * 128], tqb, kb_[:, h * 128:(h + 1) * 128], [kbb])
        ACT(lambda e: e.copy(out=qT, in_=tqbf[:, 0:512]), r=[tqb], w=[qTb])
        ACT(lambda e: e.copy(out=kT, in_=tqbf[:, 512:1024]), r=[tqb], w=[kTb])
        tp, tpb = B4[3]
        tpbf = tp.bitcast(BF16)
        for kc in range(3):
            TR(tpbf[:, kc * 128:(kc + 1) * 128], tpb, cqn[:, kc * 128:(kc + 1) * 128], [cqnb])
        cq3_ = cqnT.rearrange("p (k t) -> p k t", k=3)
        V(lambda e: e.tensor_copy(out=cq3_[:, :, ib * 128:(ib + 1) * 128], in_=tpbf[:, 0:384].rearrange("p (k t) -> p k t", k=3)),
          r=[tpb], w=[cqnTb])
        s_ps, s_b = B4[0]
        for h in range(4):
            MM(s_ps[:, h * 128:(h + 1) * 128], s_b, kT[:, h * 128:(h + 1) * 128], qT[:, h * 128:(h + 1) * 128], [kTb, qTb])
        V(lambda e: e.tensor_tensor(out=PT.rearrange("p (h c) -> p h c", h=4), in0=s_ps.rearrange("p (h c) -> p h c", h=4),
                                    in1=cmask.unsqueeze(1).to_broadcast([128, 4, 128]), op=ALU.mult), r=[s_b, constb], w=[PTb])
        y_ps, y_b = B4[1]
        for h in range(4):
            MM(y_ps[:, h * 128:(h + 1) * 128], y_b, PT[:, h * 128:(h + 1) * 128], vb_[:, h * 128:(h + 1) * 128], [PTb, vbb], start=True, stop=False)
            MM(y_ps[:, h * 128:(h + 1) * 128], y_b, qT[:, h * 128:(h + 1) * 128], Rsel_bf[:, s_ * 512 + h * 128:s_ * 512 + (h + 1) * 128], [qTb, Rselbfb], start=False, stop=True)
        for h in range(4):
            V(lambda e, h=h: e.bn_stats(out=gnst[:, h * 6:(h + 1) * 6], in_=y_ps[:, h * 128:(h + 1) * 128]), r=[y_b], w=[gnstb])
        for h in range(4):
            V(lambda e, h=h: e.bn_aggr(out=gnst[:, 24 + 2 * h:26 + 2 * h], in_=gnst[:, h * 6:(h + 1) * 6]), r=[gnstb], w=[gnstb])
        mv = gnst[:, 24:32].rearrange("p (h t) -> p h t", h=4)
        ACT(lambda e: e.activation(out=gnst[:, 0:4], in_=mv[:, :, 1], func=AF.Sqrt, bias=epsc, scale=1.0), r=[gnstb, constb], w=[gnstb])
        V(lambda e: e.reciprocal(out=gnst[:, 4:8], in_=gnst[:, 0:4]), r=[gnstb], w=[gnstb])
        for h in range(4):
            V(lambda e, h=h: e.tensor_scalar(out=yn[:, h * 128:(h + 1) * 128], in0=y_ps[:, h * 128:(h + 1) * 128],
                                             scalar1=gnst[:, 24 + 2 * h:25 + 2 * h], scalar2=gnst[:, 4 + h:5 + h], op0=ALU.subtract, op1=ALU.mult),
              r=[y_b, gnstb], w=[ynb])
        G(lambda e: e.tensor_tensor(out=ro, in0=yn, in1=sg, op=ALU.mult), r=[ynb, sgb], w=[rob])
        tr_, trb = B4[2]
        trbf = tr_.bitcast(BF16)
        for h in range(4):
            TR(trbf[:, h * 128:(h + 1) * 128], trb, ro[:, h * 128:(h + 1) * 128], [rob])
        rt3_ = retT.rearrange("p (k t) -> p k t", k=4)
        ACT(lambda e: e.copy(out=rt3_[:, :, ib * 128:(ib + 1) * 128], in_=trbf[:, 0:512].rearrange("p (k t) -> p k t", k=4)),
            r=[trb], w=[retTb])

    def B_elem(n):
        kind, idx, g, i = seq[n]
        if kind == "all" and i == 0:
            drip(10 ** 6)
            if g + 1 < ng_run:
                defer["on"] = True
                tables(g + 1)
                defer["on"] = False
        (B_elem_all if kind == "all" else B_elem_own)(n)
        if kind == "all":
            drip(7)

    def B_pe(n):
        (B_pe_all if seq[n][0] == "all" else B_pe_own)(n)

    if NSEQ:
        tables(0)
    for fn, k in ((S1, 0), (S1, 1), (S2, 0), (S1, 2), (S2, 1), (S3, 0)):
        if k < NSEQ:
            fn(k)
    for n in range(NSEQ):
        if n > 0:
            B_state(n - 1)
        if n + 3 < NSEQ:
            S1(n + 3)
        if n + 2 < NSEQ:
            S2(n + 2)
        B_elem(n)
        if n + 1 < NSEQ:
            S3(n + 1)
        B_pe(n)

    if stop == "ph12":
        dump(ckvnT.bitcast(F32)[:, 0:512], ckvnTb, 512)
        dump(KT_A.bitcast(F32)[:, 0:512], KT_Ab, 512)
        dump(retT.bitcast(F32)[:, 0:128], retTb, 128)
        dump(cqnT.bitcast(F32)[:, 0:128], cqnTb, 128)
        dump(Rst, Rb_, 512)
        dump(Rsel_bf.bitcast(F32), Rselbfb, 512)
        dump(TABS[0]["Cfr"], TABS[0]["tabb"], NTB * 128)
        dump(TABS[0]["Sfr"], TABS[0]["tabb"], NTB * 128)
        dump(KB[1][0].bitcast(F32), KB[1][1], 256)
        dump(VB[1][0].bitcast(F32), VB[1][1], 256)
        dump(qb.bitcast(F32), qbb, 256)
        dump(yn, ynb, 512)
        dump(sg, sgb, 512)
        return finish()
    P.barrier()
    A.top = m_ph12
    mlaT = A.bf(4 * 2048); mlaTb = Buf("mlaT")
    mlaT3 = mlaT.rearrange("p (k t) -> p k t", k=4)
    m_4a = A.top
    KT_B = A.bf(S); KT_Bb = Buf("KT_B")
    Vp = A.bf(NBLK * 2 * 66); Vpb = Buf("Vp")
    Vp4 = Vp.rearrange("p (b a d) -> p b a d", b=NBLK, a=2)
    w_ukv_sb = A.bf(2 * 1024); w_ukvb = Buf("w_ukv")
    w_ukv3 = w_ukv_sb.rearrange("p (k n) -> p k n", k=2)
    w_uq_sb = A.bf(3 * 768); w_uqb = Buf("w_uq")
    w_uq3 = w_uq_sb.rearrange("p (k n) -> p k n", k=3)
    w_rot = A.bf(3 * 768); w_rotb = Buf("w_rot")
    w_rot3 = w_rot.rearrange("p (k n) -> p k n", k=3)
    kvn = A.f32(2); qn = A.f32(3)
    QT = A.bf(2 * 2048); QTb = Buf("QT")
    QT3 = QT.rearrange("p (a t) -> p a t", a=2)
    cosT = A.f32(2048); sinT = A.f32(2048); csTb = Buf("csT")
    am = A.bf(8 * 256); amb = Buf("am")
    am3 = am.rearrange("p (i q) -> p i q", i=8)
    PTS = [(A.bf(512), Buf("PTa%d" % i)) for i in range(4)]
    onrm = A.bf(256); onrmb = Buf("onrm")
    o_sb = A.f32(256); o_sbb = Buf("o_sb")
    rden = A.f32(256); rdenb = Buf("rden")
    Cq = A.f32(NOWN * 32); Sq = A.f32(NOWN * 32); tabqb = Buf("tabq")
    padc = A.f32(96); pads = A.f32(96); padb = Buf("pad")
    T4q = [A.f32(NOWN * 16) for _ in range(4)]; t4qb = Buf("t4q")
    m_ph3 = A.top

    P.dma(w_ukv3, w_ukv.rearrange("(k p) n -> p k n", p=128), writes=[w_ukvb], eng="gpsimd")
    P.dma(w_uq3, w_uq.rearrange("(k p) n -> p k n", p=128), writes=[w_uqb], eng="gpsimd")
    kvnb = Buf("kvn")
    P.dma(kvn, mla_kv_norm.rearrange("(k p) -> p k", p=128), writes=[kvnb], allow_slow_non_contiguous=True)
    P.dma(qn, mla_q_norm.rearrange("(k p) -> p k", p=128), writes=[kvnb], allow_slow_non_contiguous=True)
    P.dma(am, amask, writes=[amb], eng="gpsimd")
    V(lambda e: e.tensor_scalar(out=am, in0=am, scalar1=1.0, scalar2=30000.0, op0=ALU.subtract, op1=ALU.mult), r=[amb], w=[amb])
    for kc in range(2):
        V(lambda e, kc=kc: e.tensor_scalar(out=w_ukv3[:, kc, :], in0=w_ukv3[:, kc, :], scalar1=kvn[:, kc:kc + 1], scalar2=None, op0=ALU.mult), r=[w_ukvb, kvnb], w=[w_ukvb])
    for kc in range(3):
        V(lambda e, kc=kc: e.tensor_scalar(out=w_uq3[:, kc, :], in0=w_uq3[:, kc, :], scalar1=qn[:, kc:kc + 1], scalar2=None, op0=ALU.mult), r=[w_uqb, kvnb], w=[w_uqb])
    G(lambda e: e.memset(w_rot, 0.0), w=[w_rotb])
    wq4 = w_uq_sb.rearrange("p (k h d) -> p k h d", k=3, h=8)
    wr4 = w_rot.rearrange("p (k h d) -> p k h d", k=3, h=8)
    for kc in range(3):
        V(lambda e, kc=kc: e.tensor_copy(out=wr4[:, kc, :, 64:80], in_=wq4[:, kc, :, 80:96]), r=[w_uqb, w_rotb], w=[w_rotb])
        V(lambda e, kc=kc: e.tensor_copy(out=wr4[:, kc, :, 80:96], in_=wq4[:, kc, :, 64:80]), r=[w_uqb, w_rotb], w=[w_rotb])
    G(lambda e: e.memset(Vp4[:, :, :, 64:65], 1.0), w=[Vpb])
    V(lambda e: e.tensor_copy(out=KT_B[64:96, :], in_=KT_A[64:96, :]), r=[KT_Ab], w=[KT_Bb])
    build_tables(pos_own_f[:, 0:NOWN], NOWN, 16, inv_m, Cq, Sq, T4q, [posb], tabqb, t4qb)
    G(lambda e: e.memset(padc, 0.0), w=[padb])
    G(lambda e: e.memset(pads, 0.0), w=[padb])
    for ib in range(NOWN):
        V(lambda e, ib=ib: e.tensor_scalar(out=padc[:, 64:96], in0=Cq[:, ib * 32:(ib + 1) * 32], scalar1=SM_SCALE, scalar2=None, op0=ALU.mult), r=[tabqb], w=[padb])
        V(lambda e, ib=ib: e.tensor_scalar(out=pads[:, 64:96], in0=Sq[:, ib * 32:(ib + 1) * 32], scalar1=SM_SCALE, scalar2=None, op0=ALU.mult), r=[tabqb], w=[padb])
        tp, tpb = bank()
        TR(tp[0:96, 0:128], tpb, padc, [padb])
        TR(tp[0:96, 128:256], tpb, pads, [padb])
        V(lambda e, tp=tp, ib=ib: e.tensor_copy(out=cosT[64:96, ib * 128:(ib + 1) * 128], in_=tp[64:96, 0:128]), r=[tpb], w=[csTb])
        V(lambda e, tp=tp, ib=ib: e.tensor_copy(out=sinT[64:96, ib * 128:(ib + 1) * 128], in_=tp[64:96, 128:256]), r=[tpb], w=[csTb])

    ck3 = ckvnT.rearrange("p (k t) -> p k t", k=2)
    cq3 = cqnT.rearrange("p (k t) -> p k t", k=3)
    wkv4 = w_ukv_sb.rearrange("p (k h t d) -> p k h t d", k=2, h=8, t=2)
    u1 = A.f32(512); u2 = A.f32(512); ub = Buf("u12")
    evac = {"n": 0}

    def evac_copy(out_ap, in_ap, r, w, scale=None):
        k = evac["n"] % 2
        evac["n"] += 1
        if scale is not None:
            ACT(lambda e: e.activation(out=out_ap, in_=in_ap, func=AF.Copy, scale=scale), r=r, w=w)
        elif k == 0:
            ACT(lambda e: e.copy(out=out_ap, in_=in_ap), r=r, w=w)
        else:
            V(lambda e: e.tensor_copy(out=out_ap, in_=in_ap), r=r, w=w)

    acc_i = {"n": 0}
    for hp in range(4):
        KTs = [(KT_A, KT_Ab), (KT_B, KT_Bb)]
        for a in range(2):
            h = 2 * hp + a
            KT, KTb_ = KTs[a]
            for tt in range(16):
                ps, pb = bank()
                for kc in range(2):
                    MM(ps[0:64, :], pb, w_ukv3[:, kc, h * 128:h * 128 + 64], ck3[:, kc, tt * 512:(tt + 1) * 512], [w_ukvb, ckvnTb], start=(kc == 0), stop=(kc == 1))
                evac_copy(KT[0:64, tt * 512:(tt + 1) * 512], ps[0:64, :], [pb], [KTb_])
        for b4 in range(16):
            ps, pb = bank()
            for j in range(4):
                blk = b4 * 4 + j
                for kc in range(2):
                    MM(ps[:, j * 128:(j + 1) * 128], pb, ck3[:, kc, blk * 128:(blk + 1) * 128], wkv4[:, kc, 2 * hp:2 * hp + 2, 1, :], [w_ukvb, ckvnTb], start=(kc == 0), stop=(kc == 1))
            evac_copy(Vp4[:, b4 * 4:(b4 + 1) * 4, :, 0:64], ps.rearrange("p (j a d) -> p j a d", j=4, a=2), [pb], [Vpb])
        for a in range(2):
            h = 2 * hp + a
            for tt in range(4):
                p1, p1b = bank()
                p2, p2b = bank()
                for kc in range(3):
                    MM(p1[0:96, :], p1b, w_uq3[:, kc, h * 96:(h + 1) * 96], cq3[:, kc, tt * 512:(tt + 1) * 512], [w_uqb, cqnTb], start=(kc == 0), stop=(kc == 2))
                for kc in range(3):
                    MM(p2[0:96, :], p2b, w_rot3[:, kc, h * 96:(h + 1) * 96], cq3[:, kc, tt * 512:(tt + 1) * 512], [w_rotb, cqnTb], start=(kc == 0), stop=(kc == 2))
                ACT(lambda e, p1=p1, a=a, tt=tt: e.activation(out=QT3[0:64, a, tt * 512:(tt + 1) * 512], in_=p1[0:64, :], func=AF.Copy, scale=SM_SCALE), r=[p1b], w=[QTb])
                V(lambda e, p1=p1, tt=tt: e.tensor_tensor(out=u1[64:96, :], in0=p1[64:96, :], in1=cosT[64:96, tt * 512:(tt + 1) * 512], op=ALU.mult), r=[p1b, csTb], w=[ub])
                V(lambda e, p2=p2, tt=tt: e.tensor_tensor(out=u2[64:96, :], in0=p2[64:96, :], in1=sinT[64:96, tt * 512:(tt + 1) * 512], op=ALU.mult), r=[p2b, csTb], w=[ub])
                V(lambda e, a=a, tt=tt: e.tensor_tensor(out=QT3[64:96, a, tt * 512:(tt + 1) * 512], in0=u1[64:96, :], in1=u2[64:96, :], op=ALU.add), r=[ub], w=[QTb])
        steps = []
        for g in range(NG):
            nkb = 8 * (g + 1)
            for a in range(2):
                acc, accb = banks[6 + (acc_i["n"] % 2)]
                acc_i["n"] += 1
                for kp in range(nkb // 2):
                    steps.append((g, a, kp, nkb, acc, accb))
        LOOK = 3
        sbanks = {}

        def emit_S(t):
            g, a, kp, nkb, acc, accb = steps[t]
            KT, KTb_ = KTs[a]
            s_ps, s_b = bank()
            sbanks[t] = (s_ps, s_b)
            for j in range(2):
                blk = 2 * kp + j
                diag = blk >= 8 * g
                MM(s_ps[:, j * 256:(j + 1) * 256], s_b, KT[0:96, blk * 128:(blk + 1) * 128], QT3[0:96, a, g * 256:(g + 1) * 256], [KTb_, QTb],
                   start=True, stop=not diag)
                if diag:
                    MM(s_ps[:, j * 256:(j + 1) * 256], s_b, ident, am3[:, blk - 8 * g, :], [constb, amb], start=False, stop=True)

        def emit_F2(g, a):
            bc, bcb = bank()
            MM(bc[0:64, 0:256], bcb, ones_t[64:65, 0:64], rden[64:65, :], [constb, rdenb])
            dst = mlaT3[a * 64:(a + 1) * 64, hp, g * 256:(g + 1) * 256]
            if a == 0:
                V(lambda e: e.tensor_tensor(out=dst, in0=o_sb[0:64, :], in1=bc[0:64, 0:256], op=ALU.mult),
                  r=[o_sbb, bcb], w=[mlaTb])
            else:
                V(lambda e: e.tensor_tensor(out=onrm[0:64, :], in0=o_sb[0:64, :], in1=bc[0:64, 0:256], op=ALU.mult), r=[o_sbb, bcb], w=[onrmb])
                V(lambda e: e.tensor_copy(out=dst, in_=onrm[0:64, :]), r=[onrmb], w=[mlaTb])

        def emit_rest(t):
            g, a, kp, nkb, acc, accb = steps[t]
            s_ps, s_b = sbanks.pop(t)
            pt, ptb = PTS[t % 4]
            ACT(lambda e: e.activation(out=pt, in_=s_ps, func=AF.Exp), r=[s_b], w=[ptb])
            for j in range(2):
                blk = 2 * kp + j
                MM(acc[0:65, 0:256], accb, Vp4[:, blk, a, 0:65], pt[:, j * 256:(j + 1) * 256], [Vpb, ptb], start=(blk == 0), stop=(blk == nkb - 1))
            if 2 * kp + 2 == nkb:
                ACT(lambda e: e.copy(out=o_sb[0:65, :], in_=acc[0:65, 0:256]), r=[accb], w=[o_sbb])
                V(lambda e: e.reciprocal(out=rden[64:65, :], in_=o_sb[64:65, :]), r=[o_sbb], w=[rdenb])
                return (g, a)
            return None

        pendF = []
        for t in range(min(LOOK, len(steps))):
            emit_S(t)
        for t in range(len(steps)):
            if t + LOOK < len(steps):
                emit_S(t + LOOK)
            for pf in pendF:
                pf[0] -= 1
            while pendF and pendF[0][0] <= 0:
                _, fg, fa = pendF.pop(0)
                emit_F2(fg, fa)
            fin = emit_rest(t)
            if fin is not None:
                pendF.append([2, fin[0], fin[1]])
        for _, fg, fa in pendF:
            emit_F2(fg, fa)

    if stop == "ph3":
        dump(mlaT.bitcast(F32), mlaTb, 4096)
        dump(QT.bitcast(F32)[:, 0:1024], QTb, 1024)
        dump(cosT[:, 0:256], csTb, 256)
        return finish()
    P.barrier()
    rot["n"] = 8
    C0 = m_carry

    def at_f32(off, cols):
        assert off + cols <= ACOLS, ("arena overflow", off + cols)
        return A.t[:, off:off + cols]

    def at_bf(off, cols):
        return at_f32(off, (cols + 1) // 2).bitcast(BF16)[:, 0:cols]

    A.top = m_4a
    w_o_sb = A.bf(8 * 1024)
    w_o3 = w_o_sb.rearrange("p (k n) -> p k n", k=8)
    gnw = A.f32(4)
    bc_post = A.f32(1024); bcb_ = Buf("bcast")
    HS = [(A.f32(1024), Buf("hsl%d" % i)) for i in range(2)]
    tmpf = A.f32(1024); tmpfb = Buf("tmpf")
    pfn = A.f32(8)
    E4a = A.top
    wg_sb = at_bf(C0, 8 * DFF)
    wpg_sb = at_bf(C0 + 11264, 8 * 1024)
    assert C0 + 11264 + 4096 <= m_ph12 - 4096
    wu_sb = at_bf(E4a, 8 * DFF)
    wp_sb = at_bf(E4a + 11264, 2 * 1024)
    SG = E4a + 12288
    O2 = SG + 2304
    wd_sb = at_bf(C0 + 15360, NFC * 1024)
    wg3 = wg_sb.rearrange("p (k n) -> p k n", k=8)
    wu3 = wu_sb.rearrange("p (k n) -> p k n", k=8)
    wd3 = wd_sb.rearrange("p (k n) -> p k n", k=NFC)
    wp3 = wp_sb.rearrange("p (k n) -> p k n", k=2)
    wpg3 = wpg_sb.rearrange("p (k n) -> p k n", k=8)
    w_ob = [Buf("w_o%d" % k) for k in range(8)]
    wgb = [Buf("wg%d" % k) for k in range(8)]
    wub = [Buf("wu%d" % k) for k in range(8)]
    wdb = [Buf("wd%d" % k) for k in range(NFC)]
    wpb = [Buf("wp%d" % k) for k in range(2)]
    wpgb = [Buf("wpg%d" % k) for k in range(8)]
    stg4a = [(at_f32(SG + i * 1408, 1408), Buf("stg4a%d" % i)) for i in range(5)]
    assert SG + 5 * 1408 <= ACOLS
    gnwb = Buf("gnw"); pfnb = Buf("pfn")
    P.dma(gnw, ret_gn_w.rearrange("(k p) -> p k", p=128), writes=[gnwb], allow_slow_non_contiguous=True)
    P.dma(pfn, pre_ffn_norm.rearrange("(k p) -> p k", p=128), writes=[pfnb], allow_slow_non_contiguous=True)
    P.dma(bc_post, post_mix_norm.partition_broadcast(128), writes=[bcb_])
    w_o_src = w_o.rearrange("(k p) n -> p k n", p=128)
    staged_load(w_o3[:, 0:4, :], w_ob[0:4], w_o_src[:, 0:4, :], 4, 1024, stg4a, 1024, scale_col=gnw, scale_buf=gnwb)
    staged_load(w_o3[:, 4:8, :], w_ob[4:8], w_o_src[:, 4:8, :], 4, 1024, stg4a, 1024)
    def sw_load_scaled(dst, dbuf, src, sc, scb):
        P.dma(dst, src, writes=[dbuf], eng="gpsimd")
        if sc is not None:
            G(lambda e: e.tensor_scalar(out=dst, in0=dst, scalar1=sc, scalar2=None, op0=ALU.mult), r=[dbuf, scb], w=[dbuf])

    wg_src = w_gate.rearrange("(k p) n -> p k n", p=128)
    wu_src = w_up.rearrange("(k p) n -> p k n", p=128)
    prefetch = []
    for k in range(8):
        prefetch.append(lambda k=k: staged_load(wg3[:, k:k + 1, :], wgb[k:k + 1], wg_src[:, k:k + 1, :], 1, DFF, stg4a, 1408, scale_col=pfn[:, k:k + 1], scale_buf=pfnb))
        prefetch.append(lambda k=k: staged_load(wu3[:, k:k + 1, :], wub[k:k + 1], wu_src[:, k:k + 1, :], 1, DFF, stg4a, 1408, scale_col=pfn[:, k:k + 1], scale_buf=pfnb))
    wpg_src = w_ple_gate.rearrange("(k p) n -> p k n", p=128)
    wp_src = w_ple_proj.rearrange("(k p) n -> p k n", p=128)
    for k in range(8):
        prefetch.append(lambda k=k: staged_load(wpg3[:, k:k + 1, :], wpgb[k:k + 1], wpg_src[:, k:k + 1, :], 1, 1024, stg4a, 1024))
    for k in range(2):
        prefetch.append(lambda k=k: staged_load(wp3[:, k:k + 1, :], wpb[k:k + 1], wp_src[:, k:k + 1, :], 1, 1024, stg4a, 1024))

    rt3 = retT.rearrange("p (k t) -> p k t", k=4)
    hsb = [Buf("hs%d" % i) for i in range(NOWN)]
    pss4 = {}

    def A4a(ib):
        xs, xb = HS[ib % 2]
        P.dma(xs, x_own[ib * 128:(ib + 1) * 128, :], writes=[xb])
        for _ in range(2):
            if prefetch:
                prefetch.pop(0)()
        pss = []
        for half in range(2):
            ps, pb = bank()
            for kc in range(4):
                MM(ps, pb, rt3[:, kc, ib * 128:(ib + 1) * 128], w_o3[:, kc, half * 512:(half + 1) * 512], [retTb, w_ob[kc]], start=(kc == 0), stop=False)
            for kc in range(4):
                MM(ps, pb, mlaT3[:, kc, ib * 128:(ib + 1) * 128], w_o3[:, 4 + kc, half * 512:(half + 1) * 512], [mlaTb, w_ob[4 + kc]], start=False, stop=(kc == 3))
            pss.append((ps, pb))
        pss4[ib] = pss

    def B4a_(ib):
        xs, xb = HS[ib % 2]
        pss = pss4.pop(ib)
        r, rb = rms_r([(pss[0][0], 512), (pss[1][0], 512)], 1024.0, [pss[0][1], pss[1][1]])
        for half in range(2):
            ps, pb = pss[half]
            V(lambda e, ps=ps, half=half: e.scalar_tensor_tensor(out=tmpf[:, half * 512:(half + 1) * 512], in0=ps, scalar=r, in1=bc_post[:, half * 512:(half + 1) * 512], op0=ALU.mult, op1=ALU.mult),
              r=[pb, rb, bcb_], w=[tmpfb])
        V(lambda e: e.tensor_tensor(out=xs, in0=xs, in1=tmpf, op=ALU.add), r=[xb, tmpfb], w=[xb])
        P.dma(hs[ib * 128:(ib + 1) * 128, :], xs, reads=[xb], writes=[hsb[ib]])

    A4a(0)
    for ib in range(NOWN):
        if ib + 1 < NOWN:
            A4a(ib + 1)
        B4a_(ib)
    while prefetch:
        prefetch.pop(0)()

    P.barrier()
    stg4c = [(at_f32(SG, 1024), Buf("stg4c")), (at_f32(SG + 1024, 1024), Buf("stg4d"))]
    gsb = at_f32(SG, 1024); gsbb = Buf("gsb")
    h2bf = at_bf(SG + 1024, 1024); h2bfb = Buf("h2bf")
    h2T = at_bf(SG + 1536, 1024); h2Tb = Buf("h2T")
    pbf = at_bf(SG + 2048, 256); pbfb = Buf("pbf")
    pT = at_bf(SG + 2176, 256); pTb = Buf("pT")
    M0 = C0 + 15360 + 11264
    bc_postffn = at_f32(M0, 1024); bc_ple = at_f32(M0 + 1024, 1024); bcb2 = Buf("bcast2")
    PB = [(at_f32(M0 + 4096, 256), Buf("pb0"))]
    sgf = at_f32(M0 + 4352, 128); sgfb = Buf("sgf")
    hnT = at_bf(M0 + 4480, 8 * 128); hnTb = Buf("hnT")
    hnT3 = hnT.rearrange("p (k t) -> p k t", k=8)
    assert M0 + 4992 <= E4a, (M0 + 4992, E4a)
    ACTT = []
    for i in range(2):
        a_ = at_bf(O2 + i * 1408, NFC * 128)
        ACTT.append((a_.rearrange("p (f t) -> p f t", f=NFC), Buf("actT%d" % i)))
    H2 = [(at_f32(M0 + 2048, 1024), Buf("h2_0")), (at_f32(M0 + 3072, 1024), Buf("h2_1")),
          (at_f32(O2 + 2816, 1024), Buf("h2_2")), (at_f32(O2 + 3840, 1024), Buf("h2_3"))]
    assert O2 + 4864 <= ACOLS, (O2 + 4864, ACOLS)
    b_row = gam.bitcast(BF16)[:, 0:1024]; browb = Buf("b_row")

    def hn_for(ib):
        if ib < 3:
            return H2[3][0][:, 0:512].bitcast(BF16), H2[3][1]
        return h2bf, h2bfb

    for ap_, src in ((bc_postffn, post_ffn_norm), (bc_ple, ple_norm)):
        P.dma(ap_, src.partition_broadcast(128), writes=[bcb2])
    P.dma(H2[2][0][0:1, :], b_ple_gate.rearrange("(o n) -> o n", o=1), writes=[H2[2][1]])
    V(lambda e: e.tensor_copy(out=b_row[0:1, :], in_=H2[2][0][0:1, :]), r=[H2[2][1]], w=[browb])
    wd_src = w_down.rearrange("(k p) n -> p k n", p=128)
    wd_jobs = [lambda k=k: staged_load(wd3[:, k:k + 1, :], wdb[k:k + 1], wd_src[:, k:k + 1, :], 1, 1024, stg4c, 1024) for k in range(NFC)]

    def stage_A4s(ib):
        hx, hxb = H2[ib % 4]
        hn_, hnb_ = hn_for(ib)
        P.dma(hx, hs[ib * 128:(ib + 1) * 128, :], reads=[hsb[ib]], writes=[hxb])
        r, rb = rms_r([(hx, 1024)], 1024.0, [hxb])
        V(lambda e: e.tensor_scalar(out=hn_, in0=hx, scalar1=r, scalar2=None, op0=ALU.mult), r=[hxb, rb], w=[hnb_])

    def stage_A4(ib):
        hn_, hnb_ = hn_for(ib)
        ps, pb = bank()
        psb = ps.bitcast(BF16)
        for kc in range(8):
            TR(psb[:, kc * 128:(kc + 1) * 128], pb, hn_[:, kc * 128:(kc + 1) * 128], [hnb_])
        ACT(lambda e: e.copy(out=hnT, in_=psb), r=[pb], w=[hnTb])
        actT3, actTb = ACTT[ib % 2]
        for fc in range(NFC):
            ps, pb = bank()
            for kc in range(8):
                MM(ps[:, 0:128], pb, wg3[:, kc, fc * 128:(fc + 1) * 128], hnT3[:, kc, :], [wgb[kc], hnTb], start=(kc == 0), stop=(kc == 7))
            for kc in range(8):
                MM(ps[:, 128:256], pb, wu3[:, kc, fc * 128:(fc + 1) * 128], hnT3[:, kc, :], [wub[kc], hnTb], start=(kc == 0), stop=(kc == 7))
            ACT(lambda e, ps=ps: e.activation(out=sgf, in_=ps[:, 0:128], func=AF.Silu), r=[pb], w=[sgfb])
            V(lambda e, ps=ps, fc=fc: e.tensor_tensor(out=actT3[:, fc, :], in0=sgf, in1=ps[:, 128:256], op=ALU.mult), r=[sgfb, pb], w=[actTb])

    def stage_B4a(ib):
        hx, hxb = H2[ib % 4]
        pp, ppb = PB[0]
        P.dma(pp, p_own[ib * 128:(ib + 1) * 128, :], writes=[ppb])
        actT3, actTb = ACTT[ib % 2]
        pss = []
        for half in range(2):
            ps, pb = bank()
            for fc in range(NFC):
                MM(ps, pb, actT3[:, fc, :], wd3[:, fc, half * 512:(half + 1) * 512], [actTb, wdb[fc]], start=(fc == 0), stop=(fc == NFC - 1))
            pss.append((ps, pb))
        r, rb = rms_r([(pss[0][0], 512), (pss[1][0], 512)], 1024.0, [pss[0][1], pss[1][1]])
        for half in range(2):
            ps, pb = pss[half]
            V(lambda e, ps=ps, half=half: e.scalar_tensor_tensor(out=gsb[:, half * 512:(half + 1) * 512], in0=ps, scalar=r, in1=bc_postffn[:, half * 512:(half + 1) * 512], op0=ALU.mult, op1=ALU.mult),
              r=[pb, rb, bcb2], w=[gsbb])
        G(lambda e: e.tensor_tensor(out=hx, in0=hx, in1=gsb, op=ALU.add), r=[hxb, gsbb], w=[hxb])

    def stage_B4b(ib):
        hx, hxb = H2[ib % 4]
        pp, ppb = PB[0]
        ACT(lambda e: e.copy(out=h2bf, in_=hx), r=[hxb], w=[h2bfb])
        ps, pb = bank()
        psb = ps.bitcast(BF16)
        for kc in range(8):
            TR(psb[:, kc * 128:(kc + 1) * 128], pb, h2bf[:, kc * 128:(kc + 1) * 128], [h2bfb])
        V(lambda e: e.tensor_copy(out=h2T, in_=psb), r=[pb], w=[h2Tb])
        h2T3 = h2T.rearrange("p (k t) -> p k t", k=8)
        for half in range(2):
            ps, pb = bank()
            for kc in range(8):
                MM(ps, pb, h2T3[:, kc, :], wpg3[:, kc, half * 512:(half + 1) * 512], [h2Tb, wpgb[kc]], start=(kc == 0), stop=False)
            MM(ps, pb, cmask[0:1, 0:128], b_row[0:1, half * 512:(half + 1) * 512], [constb, browb], start=False, stop=True)
            ACT(lambda e, ps=ps, half=half: e.activation(out=gsb[:, half * 512:(half + 1) * 512], in_=ps, func=AF.Sigmoid), r=[pb], w=[gsbb])
        V(lambda e: e.tensor_copy(out=pbf, in_=pp), r=[ppb], w=[pbfb])
        tp, tpb = bank()
        tpbf = tp.bitcast(BF16)
        for kc in range(2):
            TR(tpbf[:, kc * 128:(kc + 1) * 128], tpb, pbf[:, kc * 128:(kc + 1) * 128], [pbfb])
        V(lambda e: e.tensor_copy(out=pT, in_=tpbf[:, 0:256]), r=[tpb], w=[pTb])
        pT3 = pT.rearrange("p (k t) -> p k t", k=2)
        pse = []
        for half in range(2):
            ps, pb = bank()
            for kc in range(2):
                MM(ps, pb, pT3[:, kc, :], wp3[:, kc, half * 512:(half + 1) * 512], [pTb, wpb[kc]], start=(kc == 0), stop=(kc == 1))
            pse.append((ps, pb))
        r2, rb2 = rms_r([(pse[0][0], 512), (pse[1][0], 512)], 1024.0, [pse[0][1], pse[1][1]])
        for half in range(2):
            ps, pb = pse[half]
            V(lambda e, ps=ps, half=half: e.scalar_tensor_tensor(out=gsb[:, half * 512:(half + 1) * 512], in0=ps, scalar=r2, in1=gsb[:, half * 512:(half + 1) * 512], op0=ALU.mult, op1=ALU.mult),
              r=[pb, rb2, gsbb], w=[gsbb])
        V(lambda e: e.tensor_tensor(out=gsb, in0=gsb, in1=bc_ple, op=ALU.mult), r=[gsbb, bcb2], w=[gsbb])
        V(lambda e: e.tensor_tensor(out=gsb, in0=hx, in1=gsb, op=ALU.add), r=[hxb, gsbb], w=[gsbb])
        P.dma(out[ib * 128:(ib + 1) * 128, :], gsb, reads=[gsbb], is_output=True)

    stage_A4s(0)
    stage_A4(0)
    for j in wd_jobs:
        j()
    stage_A4s(1)
    stage_A4(1)
    stage_A4s(2)
    for ib in range(NOWN):
        stage_B4a(ib)
        if ib + 2 < NOWN:
            stage_A4(ib + 2)
        stage_B4b(ib)
        if ib + 3 < NOWN:
            stage_A4s(ib + 3)

    return finish()


_NC_CACHE = {}


def _rope_inv():
    inv_r = (1.0 / (np.float32(10000.0) ** (np.arange(64, dtype=np.float32) / np.float32(64)))).astype(np.float32)
    inv_m = (1.0 / (np.float32(10000.0) ** (np.arange(16, dtype=np.float32) / np.float32(16)))).astype(np.float32)
    return np.ascontiguousarray(np.broadcast_to(np.concatenate([inv_r, inv_m])[None, :], (128, 80))).astype(np.float32)


def _own_blocks(j):
    blks = []
    for g in range(NG):
        blks.append(8 * g + j)
        blks.append(8 * g + 7 - j)
    return blks


def kernel(**inputs):
    x = np.ascontiguousarray(np.asarray(inputs["x"], dtype=np.float32))
    p = np.ascontiguousarray(np.asarray(inputs["p"], dtype=np.float32))[0]
    positions = np.asarray(inputs["positions"]).astype(np.int32)
    wnames = ["pre_mix_norm", "w_in", "ret_gn_w", "mla_q_norm", "w_uq", "mla_kv_norm", "w_ukv", "w_o", "post_mix_norm",
              "pre_ffn_norm", "w_gate", "w_up", "w_down", "post_ffn_norm", "w_ple_proj", "ple_norm", "w_ple_gate", "b_ple_gate"]
    W = {n: np.ascontiguousarray(np.asarray(inputs[n], dtype=np.float32)[0]) for n in wnames}
    if "nc" not in _NC_CACHE:
        _NC_CACHE["nc"] = build_program()
    nc = _NC_CACHE["nc"]
    in_maps = []
    owns = []
    for c in range(8):
        b, j = c // 4, c % 4
        blks = _own_blocks(j)
        owns.append((b, blks))
        rows = np.concatenate([np.arange(k * 128, (k + 1) * 128) for k in blks])
        m = {
            "x_all": x[b],
            "x_own": np.ascontiguousarray(x[b][rows]),
            "p_own": np.ascontiguousarray(p[b][rows]),
            "pos_all": np.ascontiguousarray(positions[b].reshape(NBLK, 128).T),
            "pos_own": np.ascontiguousarray(positions[b][rows].reshape(NOWN, 128).T),
        }
        ws = np.zeros((128, 16), np.float32)
        ws[:, j] = 1.0
        ws[:, 8 + 7 - j] = 1.0
        m["wsel"] = ws
        am = np.zeros((128, 8, 256), np.float32)
        tri = (np.arange(128)[:, None] <= np.arange(128)[None, :]).astype(np.float32)
        for i in range(8):
            for s_, qb_ in enumerate((j, 7 - j)):
                if i < qb_:
                    am[:, i, s_ * 128:(s_ + 1) * 128] = 1.0
                elif i == qb_:
                    am[:, i, s_ * 128:(s_ + 1) * 128] = tri
        m["amask"] = am.reshape(128, 8 * 256)
        m["rope_inv"] = _rope_inv()
        m.update(W)
        in_maps.append(m)
    res = run_bass_kernel_spmd(nc, in_maps, core_ids=list(range(8)))
    out = np.empty((2, S, D), np.float32)
    for c in range(8):
        b, blks = owns[c]
        o = res.results[c]["out"]
        for k, blk in enumerate(blks):
            out[b, blk * 128:(blk + 1) * 128, :] = o[k * 128:(k + 1) * 128, :]
    return out
```
